# Optimizing a Trainium2 kernel written in Bass

```python
import math
import jax
import jax.numpy as jnp
from jax import lax
import numpy as np

D_MODEL = 4096
BATCH = 4
SEQ = 2048
DEPTH = 4
DEC_BATCH = 128
DEC_SEQ = 1
PAST_LEN = 16384
PAGE_SIZE = 128

N_MIXERS = 3
S5_GROUP = 16
S5_GROUPS = D_MODEL // S5_GROUP
S5_STATE = 64
S5_CHUNK = 128
CONV_WIDTH = 3
HG_DK = 128
HG_HEADS = D_MODEL // HG_DK
HG_DV = D_MODEL // HG_HEADS
HG_CHUNK = 32
D_FF = -(-(8 * D_MODEL) // (3 * 256)) * 256
N_S5_LAYERS = (DEPTH + N_MIXERS - 1) // N_MIXERS
N_CONV_LAYERS = (DEPTH + N_MIXERS - 2) // N_MIXERS
N_HGRN_LAYERS = (DEPTH + N_MIXERS - 3) // N_MIXERS
RMS_EPS = 1e-6

kernel_name = 'hybrid_s5_shortconv_hgrn2_step'


def rmsnorm(x, w):
    xf = x.astype(jnp.float32)
    y = xf * lax.rsqrt(jnp.mean(xf * xf, axis=-1, keepdims=True) + RMS_EPS)
    return (y * w.astype(jnp.float32)).astype(x.dtype)


def swiglu(x, w_in, w_out):
    gu = x @ w_in
    return (jax.nn.silu(gu[..., :D_FF]) * gu[..., D_FF:]) @ w_out


def _complex_combine(e1, e2):
    a1r, a1i, b1r, b1i = e1
    a2r, a2i, b2r, b2i = e2
    return (a2r * a1r - a2i * a1i, a2r * a1i + a2i * a1r,
            a2r * b1r - a2i * b1i + b2r, a2r * b1i + a2i * b1r + b2i)


def s5_mixer(u, h_re, h_im, a_re, a_im, log_dt, b_re, b_im, c_re, c_im, d_skip, w_glu):
    n, t, _ = u.shape
    f32 = jnp.float32
    ar = a_re.astype(f32)
    ai = a_im.astype(f32)
    dt = jnp.exp(log_dt.astype(f32))[:, None]
    mag = jnp.exp(ar * dt)
    lam_re = mag * jnp.cos(ai * dt)
    lam_im = mag * jnp.sin(ai * dt)
    den = ar * ar + ai * ai
    co_re = ((lam_re - 1.0) * ar + lam_im * ai) / den
    co_im = (lam_im * ar - (lam_re - 1.0) * ai) / den
    br = b_re.astype(f32)
    bi = b_im.astype(f32)
    bb_re = co_re[..., None] * br - co_im[..., None] * bi
    bb_im = co_re[..., None] * bi + co_im[..., None] * br
    cr = c_re.astype(f32)
    ci = c_im.astype(f32)
    dd = d_skip.astype(f32).reshape(S5_GROUPS, S5_GROUP)
    L = S5_CHUNK if t % S5_CHUNK == 0 else t
    nc = t // L
    uc_all = u.astype(f32).reshape(n, nc, L, S5_GROUPS, S5_GROUP).swapaxes(0, 1)

    def chunk_step(carry, uc):
        hr, hi = carry
        bu_re = jnp.einsum('nlgc,gpc->nlgp', uc, bb_re)
        bu_im = jnp.einsum('nlgc,gpc->nlgp', uc, bb_im)
        pw_re, pw_im, acc_re, acc_im = lax.associative_scan(
            _complex_combine,
            (jnp.broadcast_to(lam_re, bu_re.shape), jnp.broadcast_to(lam_im, bu_im.shape), bu_re, bu_im),
            axis=1)
        xr = pw_re * hr[:, None] - pw_im * hi[:, None] + acc_re
        xi = pw_re * hi[:, None] + pw_im * hr[:, None] + acc_im
        y = (jnp.einsum('nlgp,gcp->nlgc', xr, cr) - jnp.einsum('nlgp,gcp->nlgc', xi, ci)
             + dd * uc)
        return (xr[:, -1], xi[:, -1]), y

    (hr, hi), y = lax.scan(chunk_step, (h_re.astype(f32), h_im.astype(f32)), uc_all)
    y = y.swapaxes(0, 1).reshape(n, t, D_MODEL)
    z = jax.nn.gelu(y).astype(u.dtype)
    ab = z @ w_glu
    out = ab[..., :D_MODEL] * jax.nn.sigmoid(ab[..., D_MODEL:])
    return out, hr.astype(h_re.dtype), hi.astype(h_im.dtype)


def short_conv_mixer(x, buf, w_in, conv_w, w_out):
    t = x.shape[1]
    gb, gc, v = jnp.split(x @ w_in, 3, axis=-1)
    pre = gc * v
    xpad = jnp.concatenate([buf.astype(pre.dtype), pre], axis=1)
    conv = conv_w[0] * xpad[:, 0:t]
    for k in range(1, CONV_WIDTH):
        conv = conv + conv_w[k] * xpad[:, k:k + t]
    return (gb * conv) @ w_out, xpad[:, t:]


def hgrn2_mixer(x, s0, w_in, lb, gnorm, w_out):
    n, t, _ = x.shape
    f32 = jnp.float32
    q, fpre, iv, g = jnp.split(x @ w_in, 4, axis=-1)
    q = jax.nn.silu(q.astype(f32)).reshape(n, t, HG_HEADS, HG_DK)
    lbh = lb.astype(f32).reshape(HG_HEADS, HG_DK)
    f = lbh + (1.0 - lbh) * jax.nn.sigmoid(fpre.astype(f32).reshape(n, t, HG_HEADS, HG_DK))
    logf = jnp.log(f)
    k = 1.0 - f
    iv = iv.astype(f32).reshape(n, t, HG_HEADS, HG_DV)
    L = HG_CHUNK if t % HG_CHUNK == 0 else t
    nc = t // L

    def to_chunks(a):
        return a.reshape(n, nc, L, a.shape[2], a.shape[3]).swapaxes(0, 1)

    causal = jnp.tril(jnp.ones((L, L), dtype=bool))[None, :, :, None, None]

    def chunk_step(S, inp):
        qc, kc, lc, vc = inp
        b = jnp.cumsum(lc, axis=1)
        decay = jnp.exp(jnp.where(causal, b[:, :, None] - b[:, None, :], -jnp.inf))
        att = jnp.einsum('nlhk,nlshk,nshk->nhls', qc, decay, kc)
        o = (jnp.einsum('nlhk,nhkv->nlhv', qc * jnp.exp(b), S)
             + jnp.einsum('nhls,nshv->nlhv', att, vc))
        bl = b[:, -1]
        S = (jnp.exp(bl)[..., None] * S
             + jnp.einsum('nshk,nshv->nhkv', kc * jnp.exp(bl[:, None] - b), vc))
        return S, o

    S, o = lax.scan(chunk_step, s0.astype(f32),
                    (to_chunks(q), to_chunks(k), to_chunks(logf), to_chunks(iv)))
    o = o.swapaxes(0, 1).reshape(n, t, HG_HEADS, HG_DV)
    o = o * lax.rsqrt(jnp.mean(o * o, axis=-1, keepdims=True) + RMS_EPS)
    o = o * gnorm.astype(f32).reshape(HG_HEADS, HG_DV)
    o = o.reshape(n, t, D_MODEL) * jax.nn.sigmoid(g.astype(f32))
    return o.astype(x.dtype) @ w_out, S.astype(s0.dtype)


def trunk(x, s5_re, s5_im, conv_buf, hg_state, p):
    lb_cum = jnp.cumsum(jax.nn.softmax(p['hgrn_lb_logits'].astype(jnp.float32), axis=0), axis=0)
    out_re, out_im, out_conv, out_hg = [], [], [], []
    h = x
    for layer in range(DEPTH):
        kind, j = layer % N_MIXERS, layer // N_MIXERS
        u = rmsnorm(h, p['norm_mix'][layer])
        if kind == 0:
            m, nr, ni = s5_mixer(u, s5_re[j], s5_im[j], p['s5_a_re'][j], p['s5_a_im'][j],
                                 p['s5_log_dt'][j], p['s5_b_re'][j], p['s5_b_im'][j],
                                 p['s5_c_re'][j], p['s5_c_im'][j], p['s5_d'][j], p['s5_w_glu'][j])
            out_re.append(nr)
            out_im.append(ni)
        elif kind == 1:
            m, nb = short_conv_mixer(u, conv_buf[j], p['conv_w_in'][j], p['conv_w'][j],
                                     p['conv_w_out'][j])
            out_conv.append(nb)
        else:
            lb = lb_cum[layer] - lb_cum[0]
            m, ns = hgrn2_mixer(u, hg_state[j], p['hgrn_w_in'][j], lb, p['hgrn_gnorm'][j],
                                p['hgrn_w_out'][j])
            out_hg.append(ns)
        h = h + m
        h = h + swiglu(rmsnorm(h, p['norm_ffn'][layer]), p['ffn_w_in'][layer], p['ffn_w_out'][layer])
    return (rmsnorm(h, p['norm_final']), jnp.stack(out_re), jnp.stack(out_im),
            jnp.stack(out_conv), jnp.stack(out_hg))


def setup_inputs(seed: int = 0) -> dict:
    key = jax.random.key(seed)
    ks = jax.random.split(key, 32)
    f32 = jnp.float32

    def nrm(k, shape, s):
        return s * jax.random.normal(k, shape, f32)

    n_idx = jnp.arange(S5_STATE, dtype=f32)
    return {
        'x_prompt': nrm(ks[0], (BATCH, SEQ, D_MODEL), 1.0),
        'x_sample': nrm(ks[1], (DEC_BATCH, DEC_SEQ, D_MODEL), 1.0),
        'state_s5_re': nrm(ks[2], (N_S5_LAYERS, DEC_BATCH, S5_GROUPS, S5_STATE), 0.5),
        'state_s5_im': nrm(ks[3], (N_S5_LAYERS, DEC_BATCH, S5_GROUPS, S5_STATE), 0.5),
        'state_conv': nrm(ks[4], (N_CONV_LAYERS, DEC_BATCH, CONV_WIDTH - 1, D_MODEL), 1.0),
        'state_hgrn': nrm(ks[5], (N_HGRN_LAYERS, DEC_BATCH, HG_HEADS, HG_DK, HG_DV), 0.5),
        'norm_mix': 1.0 + nrm(ks[6], (DEPTH, D_MODEL), 0.02),
        'norm_ffn': 1.0 + nrm(ks[7], (DEPTH, D_MODEL), 0.02),
        'norm_final': 1.0 + nrm(ks[8], (D_MODEL,), 0.02),
        's5_a_re': -0.5 + nrm(ks[9], (N_S5_LAYERS, S5_GROUPS, S5_STATE), 0.01),
        's5_a_im': math.pi * n_idx + nrm(ks[10], (N_S5_LAYERS, S5_GROUPS, S5_STATE), 0.01),
        's5_log_dt': jax.random.uniform(ks[11], (N_S5_LAYERS, S5_GROUPS), f32,
                                        math.log(1e-3), math.log(1e-1)),
        's5_b_re': nrm(ks[12], (N_S5_LAYERS, S5_GROUPS, S5_STATE, S5_GROUP), S5_GROUP ** -0.5),
        's5_b_im': nrm(ks[13], (N_S5_LAYERS, S5_GROUPS, S5_STATE, S5_GROUP), S5_GROUP ** -0.5),
        's5_c_re': nrm(ks[14], (N_S5_LAYERS, S5_GROUPS, S5_GROUP, S5_STATE), (2 * S5_STATE) ** -0.5),
        's5_c_im': nrm(ks[15], (N_S5_LAYERS, S5_GROUPS, S5_GROUP, S5_STATE), (2 * S5_STATE) ** -0.5),
        's5_d': nrm(ks[16], (N_S5_LAYERS, D_MODEL), 1.0),
        's5_w_glu': nrm(ks[17], (N_S5_LAYERS, D_MODEL, 2 * D_MODEL), D_MODEL ** -0.5),
        'conv_w_in': nrm(ks[18], (N_CONV_LAYERS, D_MODEL, 3 * D_MODEL), D_MODEL ** -0.5),
        'conv_w': nrm(ks[19], (N_CONV_LAYERS, CONV_WIDTH, D_MODEL), CONV_WIDTH ** -0.5),
        'conv_w_out': nrm(ks[20], (N_CONV_LAYERS, D_MODEL, D_MODEL), D_MODEL ** -0.5),
        'hgrn_w_in': nrm(ks[21], (N_HGRN_LAYERS, D_MODEL, 4 * D_MODEL), D_MODEL ** -0.5),
        'hgrn_lb_logits': nrm(ks[22], (DEPTH, HG_HEADS * HG_DK), 0.1),
        'hgrn_gnorm': 1.0 + nrm(ks[23], (N_HGRN_LAYERS, D_MODEL), 0.02),
        'hgrn_w_out': nrm(ks[24], (N_HGRN_LAYERS, D_MODEL, D_MODEL), D_MODEL ** -0.5),
        'ffn_w_in': nrm(ks[25], (DEPTH, D_MODEL, 2 * D_FF), D_MODEL ** -0.5),
        'ffn_w_out': nrm(ks[26], (DEPTH, D_FF, D_MODEL), D_FF ** -0.5),
    }


def reference(x_prompt, x_sample, state_s5_re, state_s5_im, state_conv, state_hgrn,
              norm_mix, norm_ffn, norm_final,
              s5_a_re, s5_a_im, s5_log_dt, s5_b_re, s5_b_im, s5_c_re, s5_c_im, s5_d, s5_w_glu,
              conv_w_in, conv_w, conv_w_out,
              hgrn_w_in, hgrn_lb_logits, hgrn_gnorm, hgrn_w_out,
              ffn_w_in, ffn_w_out):
    params = dict(norm_mix=norm_mix, norm_ffn=norm_ffn, norm_final=norm_final,
                  s5_a_re=s5_a_re, s5_a_im=s5_a_im, s5_log_dt=s5_log_dt,
                  s5_b_re=s5_b_re, s5_b_im=s5_b_im, s5_c_re=s5_c_re, s5_c_im=s5_c_im,
                  s5_d=s5_d, s5_w_glu=s5_w_glu,
                  conv_w_in=conv_w_in, conv_w=conv_w, conv_w_out=conv_w_out,
                  hgrn_w_in=hgrn_w_in, hgrn_lb_logits=hgrn_lb_logits, hgrn_gnorm=hgrn_gnorm,
                  hgrn_w_out=hgrn_w_out, ffn_w_in=ffn_w_in, ffn_w_out=ffn_w_out)
    nb = x_prompt.shape[0]
    dt = x_prompt.dtype
    z_re = jnp.zeros((N_S5_LAYERS, nb, S5_GROUPS, S5_STATE), dt)
    z_im = jnp.zeros((N_S5_LAYERS, nb, S5_GROUPS, S5_STATE), dt)
    z_conv = jnp.zeros((N_CONV_LAYERS, nb, CONV_WIDTH - 1, D_MODEL), dt)
    z_hg = jnp.zeros((N_HGRN_LAYERS, nb, HG_HEADS, HG_DK, HG_DV), dt)
    y_prompt, p_re, p_im, p_conv, p_hg = trunk(x_prompt, z_re, z_im, z_conv, z_hg, params)
    y_sample, s_re, s_im, s_conv, s_hg = trunk(x_sample, state_s5_re, state_s5_im, state_conv,
                                               state_hgrn, params)
    return (y_prompt, y_sample, p_re, p_im, p_conv, p_hg, s_re, s_im, s_conv, s_hg)
```

```python
import contextlib
import math
import numpy as np
import concourse.bass as bass
import concourse.mybir as mybir
from concourse.bass_utils import run_bass_kernel_spmd

F32 = mybir.dt.float32
BF16 = mybir.dt.bfloat16
I32 = mybir.dt.int32
AF = mybir.ActivationFunctionType
ALU = mybir.AluOpType

D = 4096
KC = 32
DFF = 11008
NFC = 86
TP = 512
NS = 16
SEQ = 2048
EPS = 1e-6
NWB = 4


class Q:
    def __init__(self, nc, es, name, handle, nslots=0):
        self.h = handle
        self.name = name
        self.sem = es.enter_context(nc.semaphore("q_" + name))
        self.cnt = 0
        self.seen = {}
        self.slots = [[es.enter_context(nc.semaphore(f"d_{name}{i}")), 0] for i in range(nslots)]
        self.si = 0


class Sync:
    def __init__(self):
        self.lw = {}
        self.lr = {}

    def deps(self, q, reads, writes, acc=False):
        need = {}

        def add(t):
            if t is None:
                return
            s, v = t
            k = id(s)
            if k not in need or need[k][1] < v:
                need[k] = (s, v)
        for k in reads:
            add(self.lw.get(k))
        for k in writes:
            t = self.lw.get(k)
            if not (acc and t is not None and t[0] is q.sem):
                add(t)
            for t in self.lr.get(k, {}).values():
                add(t)
        for k, (s, v) in need.items():
            if q.seen.get(k, 0) < v:
                q.h.wait_ge(s, v)
                q.seen[k] = v

    def commit(self, tok, reads, writes):
        for k in reads:
            self.lr.setdefault(k, {})[id(tok[0])] = tok
        for k in writes:
            self.lw[k] = tok
            self.lr[k] = {}

    def op(self, q, fn, reads=(), writes=(), acc=False):
        self.deps(q, reads, writes, acc)
        inst = fn()
        q.cnt += 1
        inst.then_inc(q.sem, 1)
        tok = (q.sem, q.cnt)
        self.commit(tok, reads, writes)
        return tok

    def group(self, q, fns, reads=(), writes=(), acc=False):
        self.deps(q, reads, writes, acc)
        inst = None
        for f in fns:
            inst = f()
        q.cnt += 1
        inst.then_inc(q.sem, 1)
        tok = (q.sem, q.cnt)
        self.commit(tok, reads, writes)
        return tok

    def dma(self, q, out, in_, reads=(), writes=(), **kw):
        slot = q.slots[q.si % len(q.slots)]
        q.si += 1
        sem, prev = slot
        if prev > 0 and q.seen.get(id(sem), 0) < prev:
            q.h.wait_ge(sem, prev)
            q.seen[id(sem)] = prev
        self.deps(q, reads, writes)
        inst = q.h.dma_start(out=out, in_=in_, **kw)
        inst.then_inc(sem, 16)
        slot[1] = prev + 16
        tok = (sem, prev + 16)
        self.commit(tok, reads, writes)
        return tok


PI = math.pi


class Carver:
    def __init__(self, regions):
        self.regions = regions
        self.ri = 0
        self.off = 0

    def f32(self, n, shape=None):
        while self.off + n > self.regions[self.ri].shape[1]:
            self.ri += 1
            self.off = 0
        ap = self.regions[self.ri][:, self.off:self.off + n]
        self.off += n
        if shape is not None:
            names = " ".join(f"d{i}" for i in range(len(shape)))
            kw = {f"d{i}": s for i, s in enumerate(shape)}
            ap = ap.rearrange(f"p ({names}) -> p {names}", **kw)
        return ap

    def mark(self):
        return (self.ri, self.off)

    def reset(self, m):
        self.ri, self.off = m

    def bf16(self, n, shape=None):
        assert n % 2 == 0
        ap = self.f32(n // 2).bitcast(BF16)
        if shape is not None:
            names = " ".join(f"d{i}" for i in range(len(shape)))
            kw = {f"d{i}": s for i, s in enumerate(shape)}
            ap = ap.rearrange(f"p ({names}) -> p {names}", **kw)
        return ap


def build(npass=4, layers=(0, 1, 2, 3), do_ffn=True):
    nc = bass.Bass("TRN2", target_bir_lowering=False)

    def din(name, shape):
        return nc.dram_tensor(name, list(shape), F32, kind="ExternalInput").ap()

    def dout(name, shape):
        return nc.dram_tensor(name, list(shape), F32, kind="ExternalOutput").ap()

    xp = din("xp", [SEQ, D]); xs = din("xs", [NS, D])
    i_s5re = din("i_s5re", [2, NS, 256, 64]); i_s5im = din("i_s5im", [2, NS, 256, 64])
    i_conv = din("i_conv", [NS, 2, D]); i_hg = din("i_hg", [NS, 32, 128, 128])
    norm_mix = din("norm_mix", [4, D]); norm_ffn = din("norm_ffn", [4, D]); norm_final = din("norm_final", [1, D])
    a_re = din("s5_a_re", [2, 256, 64]); a_im = din("s5_a_im", [2, 256, 64]); log_dt = din("s5_log_dt", [2, 256])
    b_re = din("s5_b_re", [2, 256, 64, 16]); b_im = din("s5_b_im", [2, 256, 64, 16])
    c_re = din("s5_c_re", [2, 256, 16, 64]); c_im = din("s5_c_im", [2, 256, 16, 64])
    s5_d = din("s5_d", [2, D]); w_glu = din("s5_w_glu_t", [2, 2 * D, D])
    conv_w_in = din("conv_w_in_t", [3 * D, D]); conv_w = din("conv_w", [3, D]); conv_w_out = din("conv_w_out_t", [D, D])
    hg_w_in = din("hgrn_w_in_t", [4 * D, D]); hg_lb = din("hgrn_lb_logits", [4, D]); hg_gn = din("hgrn_gnorm", [1, D])
    hg_w_out = din("hgrn_w_out_t", [D, D])
    ffn_w_in = din("ffn_w_in_t", [4, 2 * DFF, D]); ffn_woA = din("ffn_w_out_ta", [4, 2, D, D]); ffn_woB = din("ffn_w_out_tb", [4, D, 22 * 128])

    yp = dout("yp", [SEQ, D]); ys = dout("ys", [NS, D])
    o_pre = dout("o_pre", [2, 256, 64]); o_pim = dout("o_pim", [2, 256, 64])
    o_pconv = dout("o_pconv", [2, D]); o_phg = dout("o_phg", [32, 128, 128])
    o_sre = dout("o_sre", [2, NS, 256, 64]); o_sim = dout("o_sim", [2, NS, 256, 64])
    o_sconv = dout("o_sconv", [NS, 2, D]); o_shg = dout("o_shg", [NS, 32, 128, 128])

    s5w = nc.dram_tensor("s5w", [2, 4, 128, KC * 128], BF16, kind="Internal").ap()
    s5t = nc.dram_tensor("s5t", [2, 128, 5 * 128], F32, kind="Internal").ap()
    s5tab = nc.dram_tensor("s5tab", [2, 128, 128, 4 * 512], F32, kind="Internal").ap()
    hgS = nc.dram_tensor("hgS", [32, 128, 128], F32, kind="Internal").ap()

    es = contextlib.ExitStack()
    with es:
        def sb(name, shape, dt=F32):
            return es.enter_context(nc.sbuf_tensor(name, list(shape), dt))

        def pst(name, shape, dt=F32):
            return es.enter_context(nc.psum_tensor(name, list(shape), dt))

        h = sb("h", [128, KC, 528])
        u = sb("u", [128, KC, 528], BF16)
        z = sb("z", [128, KC, 528], BF16)
        wb = [sb(f"wb{i}", [128, KC, 128], BF16) for i in range(NWB)]
        scr = sb("scr", [128, 4 * 532])
        tf = [scr[:, i * 532:(i + 1) * 532] for i in range(4)]
        ostage = scr[:, 0:2048].rearrange("p (a b) -> p a b", b=512)
        ost_flat = scr[:, 0:2048]
        TFK = ("tf0", "tf1", "tf2", "tf3")
        MXN = 5600
        mx = sb("mx", [128, MXN])
        sq = [sb(f"sq{i}", [128, 528], BF16) for i in range(2)]
        rstd = sb("rstd", [128, 528])
        ident = sb("ident", [128, 128])
        identb = sb("identb", [128, 128], BF16)
        ones_b = sb("ones_b", [128, 128], BF16)
        onesf = sb("onesf", [128, 128])
        pstage = sb("pstage", [128, 128])
        pc_nmix = sb("pc_nmix", [128, 128]); pc_nffn = sb("pc_nffn", [128, 128]); pc_nfin = sb("pc_nfin", [128, 32])
        pc_convw = sb("pc_convw", [128, 96]); pc_s5d = sb("pc_s5d", [128, 64]); pc_gn = sb("pc_gn", [128, 32])
        pc_lb = sb("pc_lb", [128, 32]); pc_oml = sb("pc_oml", [128, 32])
        ccarry = sb("ccarry", [128, 2, KC])
        hst = [[sb(f"hst{j}{r}", [128, 128]) for r in range(2)] for j in range(2)]
        epsc = sb("epsc", [128, 1])
        msk = sb("msk", [128, 512])
        cmask = sb("cmask", [128, 128])

        psD = [pst(f"psD{i}", [128, 1024]) for i in range(3)]
        psM = [pst(f"psM{i}", [128, 512]) for i in range(2)]

        V = Q(nc, es, "v", nc.vector)
        A = Q(nc, es, "a", nc.scalar)
        G = Q(nc, es, "g", nc.gpsimd, nslots=8)
        P = Q(nc, es, "p", nc.tensor)
        S = Q(nc, es, "s", nc.sync, nslots=8)
        QS = (V, A, G, P, S)
        sy = Sync()
        es.enter_context(nc.Block())

        def barrier():
            for q in QS:
                for o in QS:
                    if o is not q and o.cnt > q.seen.get(id(o.sem), 0):
                        q.h.wait_ge(o.sem, o.cnt)
                        q.seen[id(o.sem)] = o.cnt
                for o in (G, S):
                    for sem, val in o.slots:
                        if val > q.seen.get(id(sem), 0):
                            q.h.wait_ge(sem, val)
                            q.seen[id(sem)] = val

        def vop(fn, r=(), w=(), acc=False):
            return sy.op(V, fn, reads=r, writes=w, acc=acc)

        def aop(fn, r=(), w=()):
            return sy.op(A, fn, reads=r, writes=w)

        def gop(fn, r=(), w=()):
            return sy.op(G, fn, reads=r, writes=w)

        def pgrp(fns, r=(), w=(), acc=False):
            return sy.group(P, fns, reads=r, writes=w, acc=acc)

        wstate = {"n": 0}

        def wload(dram_rows_ap, nk):
            b = wstate["n"] % NWB
            wstate["n"] += 1
            sy.dma(G, wb[b][:].rearrange("p a b -> p (a b)")[:, 0:nk * 128], dram_rows_ap, reads=(), writes=(f"wb{b}",))
            return b

        def run_jobs(jobs, compute):
            issued = []
            for j in range(len(jobs)):
                while len(issued) < min(len(jobs), j + NWB - 1):
                    ap, nk, _ = jobs[len(issued)]
                    issued.append(wload(ap, nk))
                compute(j, issued[j], jobs[j][2])

        dps = {"n": 0}

        def next_psD():
            i = dps["n"] % 3
            dps["n"] += 1
            return i

        def dense_mm(unit, nk, rhs_fn, ps_i, ncols, rkeys):
            fns = []
            for k in range(nk):
                fns.append(lambda k=k: nc.tensor.matmul(psD[ps_i][:, 0:512], wb[unit][:, k, :], rhs_fn(k)[:, 0:512],
                                                        start=(k == 0), stop=(k == nk - 1)))
            if ncols > 512:
                for k in range(nk):
                    fns.append(lambda k=k: nc.tensor.matmul(psD[ps_i][:, 512:ncols], wb[unit][:, k, :], rhs_fn(k)[:, 512:ncols],
                                                            start=(k == 0), stop=(k == nk - 1)))
            pgrp(fns, r=tuple(rkeys) + (f"wb{unit}",), w=(f"psD{ps_i}",))

        def rmsnorm(wcol, ncols, make_u=True):
            for kc in range(KC):
                s_ = sq[kc % 2]
                aop(lambda kc=kc, s_=s_: nc.scalar.activation(out=s_[:, 0:ncols], in_=h[:, kc, 0:ncols], func=AF.Square),
                    r=(f"h{kc}",), w=(f"sq{kc % 2}",))
                pgrp([lambda kc=kc, s_=s_: nc.tensor.matmul(psM[0][:, 0:512], ones_b[:], s_[:, 0:512], start=(kc == 0), stop=(kc == KC - 1))],
                     r=(f"sq{kc % 2}", "const"), w=("psM0",), acc=(kc > 0))
                if ncols > 512:
                    pgrp([lambda kc=kc, s_=s_: nc.tensor.matmul(psM[1][:, 0:NS], ones_b[:], s_[:, 512:ncols], start=(kc == 0), stop=(kc == KC - 1))],
                         r=(f"sq{kc % 2}", "const"), w=("psM1",), acc=(kc > 0))
            aop(lambda: nc.scalar.activation(out=tf[0][:, 0:512], in_=psM[0][:, 0:512], func=AF.Sqrt, bias=epsc[:, 0:1], scale=1.0 / D),
                r=("psM0", "const"), w=("tf0",))
            if ncols > 512:
                aop(lambda: nc.scalar.activation(out=tf[0][:, 512:ncols], in_=psM[1][:, 0:NS], func=AF.Sqrt, bias=epsc[:, 0:1], scale=1.0 / D),
                    r=("psM1", "const"), w=("tf0",))
            vop(lambda: nc.vector.reciprocal(out=rstd[:, 0:ncols], in_=tf[0][:, 0:ncols]), r=("tf0",), w=("rstd",))
            if not make_u:
                return
            for kc in range(KC):
                vop(lambda kc=kc: nc.vector.scalar_tensor_tensor(out=u[:, kc, 0:ncols], in0=h[:, kc, 0:ncols], scalar=wcol(kc),
                                                                 in1=rstd[:, 0:ncols], op0=ALU.mult, op1=ALU.mult),
                    r=(f"h{kc}", "rstd", "const"), w=(f"u{kc}",))

        ukeys = tuple(f"u{k}" for k in range(KC))
        zkeys = tuple(f"z{k}" for k in range(KC))
        hkeys = tuple(f"h{k}" for k in range(KC))

        def ffn(layer, ncols):
            for blk in range(3):
                fc0 = blk * 32
                nf = min(NFC, fc0 + 32) - fc0
                jobs = []
                for fl in range(nf):
                    fc = fc0 + fl
                    jobs.append((ffn_w_in[layer, fc * 128:(fc + 1) * 128, :], KC, ("g", fl)))
                    jobs.append((ffn_w_in[layer, DFF + fc * 128:DFF + (fc + 1) * 128, :], KC, ("u", fl)))
                st = {}

                def comp_in(j, unit, tag):
                    kind, fl = tag
                    pi = next_psD()
                    dense_mm(unit, KC, lambda k: u[:, k, :], pi, ncols, ukeys)
                    if kind == "g":
                        st["g"] = pi
                    else:
                        pg = st["g"]
                        aop(lambda: nc.scalar.activation(out=tf[1][:, 0:ncols], in_=psD[pg][:, 0:ncols], func=AF.Silu),
                            r=(f"psD{pg}",), w=("tf1",))
                        vop(lambda: nc.vector.tensor_tensor(out=z[:, fl, 0:ncols], in0=tf[1][:, 0:ncols], in1=psD[pi][:, 0:ncols], op=ALU.mult),
                            r=("tf1", f"psD{pi}"), w=(f"z{fl}",))
                run_jobs(jobs, comp_in)
                if blk < 2:
                    jobs = [(ffn_woA[layer, blk, oc * 128:(oc + 1) * 128, :], nf, oc) for oc in range(KC)]
                else:
                    jobs = [(ffn_woB[layer, oc * 128:(oc + 1) * 128, :], nf, oc) for oc in range(KC)]

                def comp_out(j, unit, oc):
                    pi = next_psD()
                    dense_mm(unit, nf, lambda k: z[:, k, :], pi, ncols, zkeys[:nf])
                    vop(lambda: nc.vector.tensor_tensor(out=h[:, oc, 0:ncols], in0=h[:, oc, 0:ncols], in1=psD[pi][:, 0:ncols], op=ALU.add),
                        r=(f"psD{pi}", f"h{oc}"), w=(f"h{oc}",))
                run_jobs(jobs, comp_out)

        def out_proj_residual(wmat, ncols):
            jobs = [(wmat[oc * 128:(oc + 1) * 128, :], KC, oc) for oc in range(KC)]

            def comp(j, unit, oc):
                pi = next_psD()
                dense_mm(unit, KC, lambda k: z[:, k, :], pi, ncols, zkeys)
                vop(lambda: nc.vector.tensor_tensor(out=h[:, oc, 0:ncols], in0=h[:, oc, 0:ncols], in1=psD[pi][:, 0:ncols], op=ALU.add),
                    r=(f"psD{pi}", f"h{oc}"), w=(f"h{oc}",))
            run_jobs(jobs, comp)

        def transpose_cols_out(src_fn, nrows_tok, dst_dram_rows, tagkeys):
            for g4 in range(8):
                pm = g4 % 2
                fns = []
                for q4 in range(4):
                    kc = g4 * 4 + q4
                    fns.append(lambda kc=kc, q4=q4: nc.tensor.transpose(psM[pm][0:nrows_tok, q4 * 128:(q4 + 1) * 128], src_fn(kc), ident[:]))
                pgrp(fns, r=tuple(tagkeys) + ("const",), w=(f"psM{pm}",))
                aop(lambda g4=g4, pm=pm: nc.scalar.activation(out=ostage[0:nrows_tok, g4 % 4, :], in_=psM[pm][0:nrows_tok, :], func=AF.Copy),
                    r=(f"psM{pm}",), w=(f"tf{g4 % 4}",) if False else TFK)
                if g4 % 4 == 3:
                    c0 = (g4 // 4) * 2048
                    sy.dma(S, dst_dram_rows[:, c0:c0 + 2048], ost_flat[0:nrows_tok, :], reads=TFK, writes=("dram_out",))

        def load_cols(dst, dram2d, rows):
            sy.dma(S, pstage[0:rows, :], dram2d, writes=("pstage",))
            pgrp([lambda: nc.tensor.transpose(psM[0][:, 0:rows], pstage[0:rows, :], ident[0:rows, 0:rows])],
                 r=("pstage", "const"), w=("psM0",))
            vop(lambda: nc.vector.tensor_copy(out=dst[:, 0:rows], in_=psM[0][:, 0:rows]), r=("psM0",), w=("const",))

        def conv_mixer(p, ncols):
            barrier()
            cv = Carver([mx[:]])
            csb = cv.f32(2 * KC * NS, (2, KC, NS))
            cspre = cv.f32(KC * NS, (KC, NS))
            if p == 0:
                for r in range(2):
                    for half in range(2):
                        sy.dma(S, ost_flat[0:NS, :], i_conv[:, r, half * 2048:(half + 1) * 2048], writes=TFK)
                        fns = []
                        for q in range(16):
                            fns.append(lambda q=q: nc.tensor.transpose(psM[1][:, q * NS:(q + 1) * NS], ost_flat[0:NS, q * 128:(q + 1) * 128], ident[0:NS, 0:NS]))
                        pgrp(fns, r=TFK + ("const",), w=("psM1",))
                        vop(lambda r=r, half=half: nc.vector.tensor_copy(out=csb[:, r, half * 16:(half + 1) * 16, :],
                                                                         in_=psM[1][:, 0:16 * NS].rearrange("p (a b) -> p a b", b=NS)),
                            r=("psM1",), w=("csb",))
            W = conv_w_in
            jobs = []
            for oc in range(KC):
                jobs.append((W[D + oc * 128:D + (oc + 1) * 128, :], KC, ("gc", oc)))
                jobs.append((W[2 * D + oc * 128:2 * D + (oc + 1) * 128, :], KC, ("v", oc)))
                jobs.append((W[oc * 128:(oc + 1) * 128, :], KC, ("gb", oc)))
            st = {}
            pre = tf[2]
            t = tf[3]

            def comp(j, unit, tag):
                kind, oc = tag
                pi = next_psD()
                dense_mm(unit, KC, lambda k: u[:, k, :], pi, ncols, ukeys)
                st[kind] = pi
                if kind != "gb":
                    return
                pgc, pv, pgb = st["gc"], st["v"], st["gb"]
                w0 = pc_convw[:, oc:oc + 1]; w1 = pc_convw[:, 32 + oc:33 + oc]; w2 = pc_convw[:, 64 + oc:65 + oc]
                aop(lambda: nc.scalar.activation(out=tf[1][:, 0:ncols], in_=psD[pv][:, 0:ncols], func=AF.Copy), r=(f"psD{pv}",), w=("tf1",))
                vop(lambda: nc.vector.tensor_copy(out=pre[:, 0:2], in_=ccarry[:, :, oc]), r=("ccarry",), w=("tf2",))
                vop(lambda: nc.vector.tensor_tensor(out=pre[:, 2:2 + ncols], in0=tf[1][:, 0:ncols], in1=psD[pgc][:, 0:ncols], op=ALU.mult),
                    r=("tf1", f"psD{pgc}"), w=("tf2",))
                vop(lambda: nc.vector.tensor_scalar(out=t[:, 0:512], in0=pre[:, 0:512], scalar1=w0, scalar2=None, op0=ALU.mult),
                    r=("tf2", "const"), w=("tf3",))
                vop(lambda: nc.vector.scalar_tensor_tensor(out=t[:, 0:512], in0=pre[:, 1:513], scalar=w1, in1=t[:, 0:512], op0=ALU.mult, op1=ALU.add),
                    r=("tf2", "tf3", "const"), w=("tf3",))
                vop(lambda: nc.vector.scalar_tensor_tensor(out=t[:, 0:512], in0=pre[:, 2:514], scalar=w2, in1=t[:, 0:512], op0=ALU.mult, op1=ALU.add),
                    r=("tf2", "tf3", "const"), w=("tf3",))
                vop(lambda: nc.vector.tensor_copy(out=ccarry[:, :, oc], in_=pre[:, 512:514]), r=("tf2",), w=("ccarry",))
                if ncols > 512:
                    vop(lambda: nc.vector.tensor_scalar(out=t[:, 512:528], in0=csb[:, 0, oc, :], scalar1=w0, scalar2=None, op0=ALU.mult),
                        r=("csb", "const"), w=("tf3",))
                    vop(lambda: nc.vector.scalar_tensor_tensor(out=t[:, 512:528], in0=csb[:, 1, oc, :], scalar=w1, in1=t[:, 512:528], op0=ALU.mult, op1=ALU.add),
                        r=("csb", "tf3", "const"), w=("tf3",))
                    vop(lambda: nc.vector.scalar_tensor_tensor(out=t[:, 512:528], in0=pre[:, 514:530], scalar=w2, in1=t[:, 512:528], op0=ALU.mult, op1=ALU.add),
                        r=("tf2", "tf3", "const"), w=("tf3",))
                    vop(lambda: nc.vector.tensor_copy(out=cspre[:, oc, :], in_=pre[:, 514:530]), r=("tf2",), w=("cspre",))
                vop(lambda: nc.vector.tensor_tensor(out=z[:, oc, 0:ncols], in0=t[:, 0:ncols], in1=psD[pgb][:, 0:ncols], op=ALU.mult),
                    r=("tf3", f"psD{pgb}"), w=(f"z{oc}",))
            run_jobs(jobs, comp)
            if p == 0:
                transpose_cols_out(lambda kc: csb[:, 1, kc, :], NS, o_sconv[:, 0, :], ("csb",))
                transpose_cols_out(lambda kc: cspre[:, kc, :], NS, o_sconv[:, 1, :], ("cspre",))
            out_proj_residual(conv_w_out, ncols)
            barrier()

        def cmul(o_re, o_im, a_re_, a_im_, b_re_, b_im_, t1, t2, k1, k2, keys_r, keys_w):
            keys_r = tuple(keys_r); keys_w = tuple(keys_w)
            vop(lambda: nc.vector.tensor_tensor(out=t1, in0=a_re_, in1=b_re_, op=ALU.mult), r=keys_r, w=(k1,))
            vop(lambda: nc.vector.tensor_tensor(out=t2, in0=a_im_, in1=b_im_, op=ALU.mult), r=keys_r, w=(k2,))
            vop(lambda: nc.vector.tensor_tensor(out=o_re, in0=t1, in1=t2, op=ALU.subtract), r=(k1, k2) + keys_r, w=keys_w)
            vop(lambda: nc.vector.tensor_tensor(out=t1, in0=a_re_, in1=b_im_, op=ALU.mult), r=keys_r + keys_w, w=(k1,))
            vop(lambda: nc.vector.tensor_tensor(out=t2, in0=a_im_, in1=b_re_, op=ALU.mult), r=keys_r + keys_w, w=(k2,))
            vop(lambda: nc.vector.tensor_tensor(out=o_im, in0=t1, in1=t2, op=ALU.add), r=(k1, k2), w=keys_w)

        def sincos(src_turns, o_sin, o_cos, ki_ap, fr, n, key):
            K_ = (key,)
            vop(lambda: nc.vector.tensor_copy(out=ki_ap, in_=src_turns), r=K_, w=K_)
            vop(lambda: nc.vector.tensor_tensor(out=fr, in0=src_turns, in1=ki_ap, op=ALU.subtract), r=K_, w=K_)
            aop(lambda: nc.scalar.activation(out=o_sin, in_=fr, func=AF.Sin, scale=2 * PI), r=K_, w=K_)
            aop(lambda: nc.scalar.activation(out=o_cos, in_=fr, func=AF.Sin, scale=PI), r=K_, w=K_)
            vop(lambda: nc.vector.tensor_tensor(out=o_cos, in0=o_cos, in1=o_cos, op=ALU.mult), r=K_, w=K_)
            vop(lambda: nc.vector.tensor_scalar(out=o_cos, in0=o_cos, scalar1=-2.0, scalar2=1.0, op0=ALU.mult, op1=ALU.add), r=K_, w=K_)

        def s5_setup(j):
            barrier()
            hflat = h[:].rearrange("p a b -> p (a b)")
            uflat = u[:].rearrange("p a b -> p (a b)").bitcast(F32)
            zflat = z[:].rearrange("p a b -> p (a b)").bitcast(F32)
            cv = Carver([hflat, uflat, zflat])
            SU = ("su",)

            def sq_(n=128):
                return cv.f32(n)
            nat = sq_(); AR = sq_(); AI = sq_(); LDT = sq_(); ldt2 = sq_(2)
            DT = sq_(); ALPHA = sq_(); THETA = sq_(); THT = sq_(); MAG = sq_(); SIN = sq_(); COS = sq_(); FR = sq_(); KI = sq_().bitcast(I32)
            LRE = sq_(); LIM = sq_(); DEN = sq_(); RDEN = sq_(); LM1 = sq_(); CORE = sq_(); COIM = sq_(); TA = sq_(); TB = sq_()
            tb5 = cv.f32(5 * 128, (5, 128))
            rm0 = sq_(1); rm1 = sq_(1)

            def su_v(fn):
                return vop(fn, r=SU, w=SU)

            def su_a(fn):
                return aop(fn, r=SU, w=SU)

            def load_T(dst, dram2d_fn):
                dram2d_fn()
                pgrp([lambda: nc.tensor.transpose(psM[0][:, 0:128], nat, ident[:])], r=SU + ("const",), w=("psM0",))
                vop(lambda: nc.vector.tensor_copy(out=dst, in_=psM[0][:, 0:128]), r=("psM0",), w=SU)

            load_T(AR, lambda: sy.dma(S, nat, a_re[j].rearrange("(i jj) p -> i (jj p)", jj=2), reads=SU, writes=SU))
            load_T(AI, lambda: sy.dma(S, nat, a_im[j].rearrange("(i jj) p -> i (jj p)", jj=2), reads=SU, writes=SU))
            sy.dma(S, ldt2, log_dt[j].rearrange("(i jj) -> i jj", jj=2), reads=SU, writes=SU)

            def fill_ldt():
                for jj in range(2):
                    su_v(lambda jj=jj: nc.vector.tensor_scalar(out=nat[:, jj * 64:(jj + 1) * 64], in0=onesf[:, 0:64], scalar1=ldt2[:, jj:jj + 1],
                                                              scalar2=None, op0=ALU.mult))
            load_T(LDT, fill_ldt)
            su_a(lambda: nc.scalar.activation(out=DT, in_=LDT, func=AF.Exp))
            su_v(lambda: nc.vector.tensor_tensor(out=ALPHA, in0=AR, in1=DT, op=ALU.mult))
            su_v(lambda: nc.vector.tensor_tensor(out=THETA, in0=AI, in1=DT, op=ALU.mult))
            su_v(lambda: nc.vector.tensor_scalar(out=THT, in0=THETA, scalar1=1.0 / (2 * PI), scalar2=None, op0=ALU.mult))
            su_a(lambda: nc.scalar.activation(out=MAG, in_=ALPHA, func=AF.Exp))
            sincos(THT, SIN, COS, KI, FR, 128, "su")
            su_v(lambda: nc.vector.tensor_tensor(out=LRE, in0=MAG, in1=COS, op=ALU.mult))
            su_v(lambda: nc.vector.tensor_tensor(out=LIM, in0=MAG, in1=SIN, op=ALU.mult))
            su_v(lambda: nc.vector.tensor_copy(out=tb5[:, 0, :], in_=LRE))
            su_v(lambda: nc.vector.tensor_copy(out=tb5[:, 1, :], in_=LIM))
            su_v(lambda: nc.vector.tensor_scalar(out=tb5[:, 2, :], in0=LIM, scalar1=-1.0, scalar2=None, op0=ALU.mult))
            su_a(lambda: nc.scalar.activation(out=TA, in_=ALPHA, func=AF.Exp, scale=float(TP - 1)))
            su_v(lambda: nc.vector.tensor_scalar(out=TB, in0=THT, scalar1=float(TP - 1), scalar2=None, op0=ALU.mult))
            sincos(TB, SIN, COS, KI, FR, 128, "su")
            su_v(lambda: nc.vector.tensor_tensor(out=tb5[:, 3, :], in0=TA, in1=COS, op=ALU.mult))
            su_v(lambda: nc.vector.tensor_tensor(out=tb5[:, 4, :], in0=TA, in1=SIN, op=ALU.mult))
            sy.dma(S, s5t[j].rearrange("p (a b) -> p a b", b=128), tb5, reads=SU, writes=("s5t",))
            su_v(lambda: nc.vector.tensor_tensor(out=DEN, in0=AR, in1=AR, op=ALU.mult))
            su_v(lambda: nc.vector.tensor_tensor(out=TA, in0=AI, in1=AI, op=ALU.mult))
            su_v(lambda: nc.vector.tensor_tensor(out=DEN, in0=DEN, in1=TA, op=ALU.add))
            su_v(lambda: nc.vector.reciprocal(out=RDEN, in_=DEN))
            su_v(lambda: nc.vector.tensor_scalar(out=LM1, in0=LRE, scalar1=-1.0, scalar2=None, op0=ALU.add))
            su_v(lambda: nc.vector.tensor_tensor(out=TA, in0=LM1, in1=AR, op=ALU.mult))
            su_v(lambda: nc.vector.tensor_tensor(out=TB, in0=LIM, in1=AI, op=ALU.mult))
            su_v(lambda: nc.vector.tensor_tensor(out=TA, in0=TA, in1=TB, op=ALU.add))
            su_v(lambda: nc.vector.tensor_tensor(out=CORE, in0=TA, in1=RDEN, op=ALU.mult))
            su_v(lambda: nc.vector.tensor_tensor(out=TA, in0=LIM, in1=AR, op=ALU.mult))
            su_v(lambda: nc.vector.tensor_tensor(out=TB, in0=LM1, in1=AI, op=ALU.mult))
            su_v(lambda: nc.vector.tensor_tensor(out=TA, in0=TA, in1=TB, op=ALU.subtract))
            su_v(lambda: nc.vector.tensor_tensor(out=COIM, in0=TA, in1=RDEN, op=ALU.mult))
            mk = cv.mark()
            Braw_re = cv.f32(2048, (128, 16)); Braw_im = cv.f32(2048, (128, 16))
            bb_re = cv.f32(2048, (128, 16)); bb_im = cv.f32(2048, (128, 16)); btmp = cv.f32(2048, (128, 16))
            bbpad_flat = cv.f32(4096)
            bbpad = bbpad_flat.rearrange("p (k a j c) -> p k a j c", k=32, a=4, j=2, c=16)
            for jj in range(2):
                for i8 in range(8):
                    isl = slice(i8 * 16, (i8 + 1) * 16)
                    sy.dma(S, Braw_re[jj * 64:(jj + 1) * 64, isl, :], b_re[j].rearrange("(i jj) p c -> jj p i c", jj=2)[jj][:, isl, :], reads=SU, writes=SU)
                    sy.dma(S, Braw_im[jj * 64:(jj + 1) * 64, isl, :], b_im[j].rearrange("(i jj) p c -> jj p i c", jj=2)[jj][:, isl, :], reads=SU, writes=SU)
            core_b = CORE.unsqueeze(2).to_broadcast([128, 128, 16])
            coim_b = COIM.unsqueeze(2).to_broadcast([128, 128, 16])
            su_v(lambda: nc.vector.tensor_tensor(out=bb_re, in0=Braw_re, in1=core_b, op=ALU.mult))
            su_v(lambda: nc.vector.tensor_tensor(out=btmp, in0=Braw_im, in1=coim_b, op=ALU.mult))
            su_v(lambda: nc.vector.tensor_tensor(out=bb_re, in0=bb_re, in1=btmp, op=ALU.subtract))
            su_v(lambda: nc.vector.tensor_tensor(out=bb_im, in0=Braw_im, in1=core_b, op=ALU.mult))
            su_v(lambda: nc.vector.tensor_tensor(out=btmp, in0=Braw_re, in1=coim_b, op=ALU.mult))
            su_v(lambda: nc.vector.tensor_tensor(out=bb_im, in0=bb_im, in1=btmp, op=ALU.add))

            def emit_unit(src_fn, unit_idx, scale):
                for g in range(8):
                    pd = g % 2
                    fns = []
                    for q in range(4):
                        fns.append(lambda g=g, q=q: nc.tensor.transpose(psD[pd][:, q * 128:(q + 1) * 128], src_fn(g * 4 + q), ident[:]))
                    pgrp(fns, r=SU + ("const",), w=(f"psD{pd}",))
                    aop(lambda g=g, pd=pd: nc.scalar.mul(wb[0][:, g * 4:(g + 1) * 4, :], psD[pd][:, 0:512].rearrange("p (a b) -> p a b", b=128), scale),
                        r=(f"psD{pd}",), w=("wb0",))
                sy.dma(S, s5w[j, unit_idx], wb[0][:].rearrange("p a b -> p (a b)"), reads=("wb0",), writes=("s5w",))

            for bb, ui in ((bb_re, 0), (bb_im, 1)):
                su_v(lambda: nc.vector.memset(bbpad_flat, 0.0))
                for jj in range(2):
                    su_v(lambda jj=jj, bb=bb: nc.vector.tensor_copy(out=bbpad[jj * 64:(jj + 1) * 64, :, :, jj, :],
                                                                    in_=bb[jj * 64:(jj + 1) * 64].rearrange("p (k a) c -> p k a c", a=4)))
                emit_unit(lambda kc: bbpad[:, kc].rearrange("p a b c -> p (a b c)"), ui, 1.0)
            barrier()
            cv.reset(mk)
            rrow = sq_(128)
            gop(lambda: nc.gpsimd.memset(rrow[0:1, :], 0.0), r=SU, w=SU)
            for b4 in range(4):
                gop(lambda b4=b4: nc.gpsimd.memset(rrow[0:1, b4 * 32:b4 * 32 + 16], 1.0), r=SU, w=SU)
            pgrp([lambda: nc.tensor.transpose(psM[0][:, 0:1], rrow[0:1, :], ident[0:1, 0:1])], r=SU + ("const",), w=("psM0",))
            vop(lambda: nc.vector.tensor_copy(out=rm0, in_=psM[0][:, 0:1]), r=("psM0",), w=SU)
            su_v(lambda: nc.vector.tensor_scalar(out=rm1, in0=rm0, scalar1=-1.0, scalar2=1.0, op0=ALU.mult, op1=ALU.add))
            Cnat = cv.f32(2048, (32, 64)); Cbd = cv.f32(4096, (32, 128))
            for cdram, ui, scale in ((c_re, 2, 1.0), (c_im, 3, -1.0)):
                for g8 in range(8):
                    sy.dma(S, Cnat[g8 * 16:(g8 + 1) * 16], cdram[j].rearrange("(k g8) c p -> g8 c k p", g8=8)[g8], reads=SU, writes=SU)
                su_v(lambda: nc.vector.tensor_scalar(out=Cbd[:, :, 0:64], in0=Cnat, scalar1=rm0[:, 0:1], scalar2=None, op0=ALU.mult))
                su_v(lambda: nc.vector.tensor_scalar(out=Cbd[:, :, 64:128], in0=Cnat, scalar1=rm1[:, 0:1], scalar2=None, op0=ALU.mult))
                emit_unit(lambda kc: Cbd[:, kc, :], ui, scale)
            barrier()
            cv.reset(mk)
            iota_i = cv.f32(512).bitcast(I32); iota = cv.f32(512)
            gop(lambda: nc.gpsimd.iota(iota_i, pattern=[[1, 512]], base=0, channel_multiplier=0), r=SU, w=SU)
            su_v(lambda: nc.vector.tensor_copy(out=iota, in_=iota_i))
            tabs = [cv.f32(2048, (4, 512)) for _ in range(2)]
            Ttn = cv.f32(512); Tfr = cv.f32(512); Tki = cv.f32(512).bitcast(I32); Tsin = cv.f32(512); Tcos = cv.f32(512); Tmag = cv.f32(512); Timag = cv.f32(512)
            NAL = sq_()
            su_v(lambda: nc.vector.tensor_scalar(out=NAL, in0=ALPHA, scalar1=-1.0, scalar2=None, op0=ALU.mult))
            for i in range(128):
                tab = tabs[i % 2]
                k = f"tab{i % 2}"
                vop(lambda i=i: nc.vector.tensor_scalar(out=Ttn, in0=iota, scalar1=THT[:, i:i + 1], scalar2=None, op0=ALU.mult), r=SU, w=("tt",))
                vop(lambda: nc.vector.tensor_copy(out=Tki, in_=Ttn), r=("tt",), w=("tki",))
                vop(lambda: nc.vector.tensor_tensor(out=Tfr, in0=Ttn, in1=Tki, op=ALU.subtract), r=("tt", "tki"), w=("tfr",))
                aop(lambda: nc.scalar.activation(out=Tsin, in_=Tfr, func=AF.Sin, scale=2 * PI), r=("tfr",), w=("tsin",))
                aop(lambda: nc.scalar.activation(out=Tcos, in_=Tfr, func=AF.Sin, scale=PI), r=("tfr",), w=("tcos",))
                aop(lambda i=i: nc.scalar.activation(out=Tmag, in_=iota, func=AF.Exp, scale=ALPHA[:, i:i + 1]), r=SU, w=("tmag",))
                aop(lambda i=i: nc.scalar.activation(out=Timag, in_=iota, func=AF.Exp, scale=NAL[:, i:i + 1]), r=SU, w=("timag",))
                gop(lambda: nc.gpsimd.tensor_tensor(out=Tcos, in0=Tcos, in1=Tcos, op=ALU.mult), r=("tcos",), w=("tcos",))
                gop(lambda: nc.gpsimd.tensor_scalar(out=Tcos, in0=Tcos, scalar1=-2.0, scalar2=1.0, op0=ALU.mult, op1=ALU.add), r=("tcos",), w=("tcos",))
                vop(lambda tab=tab: nc.vector.tensor_tensor(out=tab[:, 0, :], in0=Tcos, in1=Timag, op=ALU.mult), r=("tcos", "timag"), w=(k,))
                vop(lambda tab=tab: nc.vector.scalar_tensor_tensor(out=tab[:, 1, :], in0=Tsin, scalar=-1.0, in1=Timag, op0=ALU.mult, op1=ALU.mult),
                    r=("tsin", "timag", k), w=(k,))
                gop(lambda tab=tab: nc.gpsimd.tensor_tensor(out=tab[:, 2, :], in0=Tcos, in1=Tmag, op=ALU.mult), r=("tcos", "tmag", k), w=(k,))
                gop(lambda tab=tab: nc.gpsimd.tensor_tensor(out=tab[:, 3, :], in0=Tsin, in1=Tmag, op=ALU.mult), r=("tsin", "tmag", k), w=(k,))
                sy.dma(S, s5tab[j, i].rearrange("p (a b) -> p a b", b=512), tab, reads=(k,), writes=("s5tab",))
            barrier()

        def s5_mixer(j, layer, p, ncols):
            barrier()
            cv = Carver([mx[:]])
            Gt = cv.f32(1024, (2, 512)); Ft = cv.f32(1024, (2, 512))
            X1 = cv.bf16(528); X2 = cv.bf16(528)
            tb5 = cv.f32(5 * 128, (5, 128))
            cre = cv.f32(128); cim = cv.f32(128); zlre = cv.f32(128); zlim = cv.f32(128)
            stg = [cv.f32(512, (4, 128)) for _ in range(2)]
            hsb = [cv.f32(64, (4, NS)) for _ in range(2)]
            xsn = [cv.f32(64, (4, NS)) for _ in range(2)]
            uz = [cv.bf16(528) for _ in range(2)]
            for q in range(2):
                vop(lambda q=q: nc.vector.memset(uz[q][64:96, :], 0.0), w=(f"uz{q}",))
            W1, W2, W3, W4 = tf
            LRE, LIM, NLIM, FLRE, FLIM = (tb5[:, q, :] for q in range(5))
            hre, him = hst[j][0][:], hst[j][1][:]
            i_st = (i_s5re, i_s5im); o_st = (o_sre, o_sim)
            for q in range(4):
                sy.dma(S, wb[q][:].rearrange("p a b -> p (a b)"), s5w[j, q], reads=("s5w",), writes=(f"wb{q}",))
            sy.dma(S, tb5, s5t[j].rearrange("p (a b) -> p a b", b=128), reads=("s5t",), writes=("tb5",))
            if p == 0:
                vop(lambda: nc.vector.memset(cre, 0.0), w=("cc",))
                vop(lambda: nc.vector.memset(cim, 0.0), w=("cc",))
            else:
                cmul(cre, cim, LRE, LIM, hre, him, W3[:, 0:128], W4[:, 0:128], "tf2", "tf3", ("tb5", f"hst{j}"), ("cc",))
            dcol = lambda kc: pc_s5d[:, j * 32 + kc:j * 32 + kc + 1]
            wcol = lambda kc: pc_nmix[:, layer * 32 + kc:layer * 32 + kc + 1]
            for kc in range(KC):
                if p == 0:
                    for r in range(2):
                        sy.dma(S, stg[r][0:NS], i_st[r][j].rearrange("n (i jj) p -> n i (jj p)", jj=2)[:, kc * 4:(kc + 1) * 4, :], writes=(f"stg{r}",))
                        pgrp([lambda r=r, q=q: nc.tensor.transpose(psM[r][:, q * NS:(q + 1) * NS], stg[r][0:NS, q, :], ident[0:NS, 0:NS]) for q in range(4)],
                             r=(f"stg{r}", "const"), w=(f"psM{r}",))
                        vop(lambda r=r: nc.vector.tensor_copy(out=hsb[r], in_=psM[r][:, 0:4 * NS].rearrange("p (a b) -> p a b", b=NS)),
                            r=(f"psM{r}",), w=(f"hsb{r}",))
                uzt = uz[kc % 2]
                uzk = f"uz{kc % 2}"
                sy.dma(S, uzt[96:128, 0:ncols], u[96:128, kc, 0:ncols], reads=(f"u{kc}",), writes=(uzk,))
                for i4 in (0, 1, 3, 2):
                    i = kc * 4 + i4
                    a = i % 2
                    if i4 < 3:
                        rows = slice(i4 * 32, (i4 + 1) * 32)
                        usrc = u[rows, kc, :]
                        ukey = f"u{kc}"
                    else:
                        rows = slice(64, 128)
                        usrc = uzt[rows, :]
                        ukey = uzk
                    sy.dma(S, Gt, s5tab[j, i][:, 0:1024].rearrange("p (a b) -> p a b", b=512), reads=("s5tab",), writes=("Gt",))
                    sy.dma(S, Ft, s5tab[j, i][:, 1024:2048].rearrange("p (a b) -> p a b", b=512), reads=("s5tab",), writes=("Ft",))
                    pgrp([lambda: nc.tensor.matmul(psD[a][:, 0:512], wb[0][rows, kc, :], usrc[:, 0:512], start=True, stop=True),
                          lambda: nc.tensor.matmul(psD[a][:, 512:1024], wb[1][rows, kc, :], usrc[:, 0:512], start=True, stop=True)],
                         r=(ukey, "wb0", "wb1"), w=(f"psD{a}",))
                    bre = psD[a][:, 0:512]; bim = psD[a][:, 512:1024]
                    Gre, Gim, Fre, Fim = Gt[:, 0, :], Gt[:, 1, :], Ft[:, 0, :], Ft[:, 1, :]
                    pk = f"psD{a}"
                    vop(lambda: nc.vector.tensor_tensor(out=W1[:, 0:512], in0=bre, in1=Gre, op=ALU.mult), r=(pk, "Gt"), w=("tf0",))
                    vop(lambda: nc.vector.tensor_tensor(out=W2[:, 0:512], in0=bim, in1=Gim, op=ALU.mult), r=(pk, "Gt"), w=("tf1",))
                    gop(lambda: nc.gpsimd.tensor_tensor(out=W1[:, 0:512], in0=W1[:, 0:512], in1=W2[:, 0:512], op=ALU.subtract), r=("tf0", "tf1"), w=("tf0",))
                    vop(lambda: nc.vector.tensor_tensor(out=W2[:, 0:512], in0=bre, in1=Gim, op=ALU.mult), r=(pk, "Gt", "tf1"), w=("tf1",))
                    vop(lambda: nc.vector.tensor_tensor(out=W3[:, 0:512], in0=bim, in1=Gre, op=ALU.mult), r=(pk, "Gt"), w=("tf2",))
                    gop(lambda: nc.gpsimd.tensor_tensor(out=W2[:, 0:512], in0=W2[:, 0:512], in1=W3[:, 0:512], op=ALU.add), r=("tf1", "tf2"), w=("tf1",))
                    vop(lambda i=i: nc.vector.tensor_tensor_scan(out=W3[:, 0:512], data0=onesf[:, 0:1].to_broadcast([128, 512]), data1=W1[:, 0:512],
                                                                 initial=cre[:, i:i + 1], op0=ALU.mult, op1=ALU.add), r=("tf0", "cc", "const"), w=("tf2",))
                    vop(lambda i=i: nc.vector.tensor_tensor_scan(out=W4[:, 0:512], data0=onesf[:, 0:1].to_broadcast([128, 512]), data1=W2[:, 0:512],
                                                                 initial=cim[:, i:i + 1], op0=ALU.mult, op1=ALU.add), r=("tf1", "cc", "const"), w=("tf3",))
                    aop(lambda i=i: nc.scalar.activation(out=zlre[:, i:i + 1], in_=W3[:, 511:512], func=AF.Copy), r=("tf2",), w=("zl",))
                    aop(lambda i=i: nc.scalar.activation(out=zlim[:, i:i + 1], in_=W4[:, 511:512], func=AF.Copy), r=("tf3",), w=("zl",))
                    gop(lambda: nc.gpsimd.tensor_tensor(out=W1[:, 0:512], in0=Fre, in1=W3[:, 0:512], op=ALU.mult), r=("Ft", "tf2"), w=("tf0",))
                    gop(lambda: nc.gpsimd.tensor_tensor(out=W2[:, 0:512], in0=Fim, in1=W4[:, 0:512], op=ALU.mult), r=("Ft", "tf3"), w=("tf1",))
                    vop(lambda: nc.vector.tensor_tensor(out=X1[:, 0:512], in0=W1[:, 0:512], in1=W2[:, 0:512], op=ALU.subtract), r=("tf0", "tf1"), w=("X1",))
                    gop(lambda: nc.gpsimd.tensor_tensor(out=W1[:, 0:512], in0=Fre, in1=W4[:, 0:512], op=ALU.mult), r=("Ft", "tf3", "X1"), w=("tf0",))
                    gop(lambda: nc.gpsimd.tensor_tensor(out=W2[:, 0:512], in0=Fim, in1=W3[:, 0:512], op=ALU.mult), r=("Ft", "tf2", "X1"), w=("tf1",))
                    vop(lambda: nc.vector.tensor_tensor(out=X2[:, 0:512], in0=W1[:, 0:512], in1=W2[:, 0:512], op=ALU.add), r=("tf0", "tf1"), w=("X2",))
                    if p == 0:
                        pgrp([lambda: nc.tensor.matmul(psM[a][:, 0:NS], wb[0][rows, kc, :], usrc[:, 512:528], start=True, stop=True),
                              lambda: nc.tensor.matmul(psM[a][:, NS:2 * NS], wb[1][rows, kc, :], usrc[:, 512:528], start=True, stop=True)],
                             r=(ukey, "wb0", "wb1"), w=(f"psM{a}",))
                        pm = f"psM{a}"
                        vop(lambda i=i, i4=i4: nc.vector.scalar_tensor_tensor(out=xsn[0][:, i4, :], in0=hsb[0][:, i4, :], scalar=LRE[:, i:i + 1], in1=psM[a][:, 0:NS],
                                                                             op0=ALU.mult, op1=ALU.add), r=("hsb0", "tb5", pm), w=("xsn0",))
                        vop(lambda i=i, i4=i4: nc.vector.scalar_tensor_tensor(out=xsn[0][:, i4, :], in0=hsb[1][:, i4, :], scalar=NLIM[:, i:i + 1], in1=xsn[0][:, i4, :],
                                                                             op0=ALU.mult, op1=ALU.add), r=("hsb1", "tb5", "xsn0"), w=("xsn0",))
                        vop(lambda i=i, i4=i4: nc.vector.scalar_tensor_tensor(out=xsn[1][:, i4, :], in0=hsb[1][:, i4, :], scalar=LRE[:, i:i + 1], in1=psM[a][:, NS:2 * NS],
                                                                             op0=ALU.mult, op1=ALU.add), r=("hsb1", "tb5", pm), w=("xsn1",))
                        vop(lambda i=i, i4=i4: nc.vector.scalar_tensor_tensor(out=xsn[1][:, i4, :], in0=hsb[0][:, i4, :], scalar=LIM[:, i:i + 1], in1=xsn[1][:, i4, :],
                                                                             op0=ALU.mult, op1=ALU.add), r=("hsb0", "tb5", "xsn1"), w=("xsn1",))
                        vop(lambda i4=i4: nc.vector.tensor_copy(out=X1[:, 512:528], in_=xsn[0][:, i4, :]), r=("xsn0",), w=("X1",))
                        vop(lambda i4=i4: nc.vector.tensor_copy(out=X2[:, 512:528], in_=xsn[1][:, i4, :]), r=("xsn1",), w=("X2",))
                    cols = slice(i4 * 32, (i4 + 1) * 32) if i4 < 3 else slice(64, 128)
                    fns = [lambda: nc.tensor.matmul(psD[2][cols, 0:512], wb[2][:, kc, cols], X1[:, 0:512], start=True, stop=False),
                           lambda: nc.tensor.matmul(psD[2][cols, 0:512], wb[3][:, kc, cols], X2[:, 0:512], start=False, stop=True)]
                    if p == 0:
                        fns += [lambda: nc.tensor.matmul(psD[2][cols, 512:528], wb[2][:, kc, cols], X1[:, 512:528], start=True, stop=False),
                                lambda: nc.tensor.matmul(psD[2][cols, 512:528], wb[3][:, kc, cols], X2[:, 512:528], start=False, stop=True)]
                    pgrp(fns, r=("X1", "X2", "wb2", "wb3"), w=("psD2",), acc=(i4 > 0))
                vop(lambda kc=kc: nc.vector.scalar_tensor_tensor(out=W1[:, 0:ncols], in0=h[:, kc, 0:ncols], scalar=wcol(kc), in1=rstd[:, 0:ncols],
                                                                 op0=ALU.mult, op1=ALU.mult), r=(f"h{kc}", "rstd", "const"), w=("tf0",))
                vop(lambda kc=kc: nc.vector.scalar_tensor_tensor(out=W1[:, 0:ncols], in0=W1[:, 0:ncols], scalar=dcol(kc), in1=psD[2][:, 0:ncols],
                                                                 op0=ALU.mult, op1=ALU.add), r=("tf0", "psD2", "const"), w=("tf0",))
                gop(lambda: nc.gpsimd.tensor_tensor(out=W2[:, 0:ncols], in0=W1[:, 0:ncols], in1=W1[:, 0:ncols], op=ALU.mult), r=("tf0",), w=("tf1",))
                gop(lambda: nc.gpsimd.tensor_scalar(out=W2[:, 0:ncols], in0=W2[:, 0:ncols], scalar1=0.044715, scalar2=1.0, op0=ALU.mult, op1=ALU.add), r=("tf1",), w=("tf1",))
                gop(lambda: nc.gpsimd.tensor_tensor(out=W2[:, 0:ncols], in0=W2[:, 0:ncols], in1=W1[:, 0:ncols], op=ALU.mult), r=("tf0", "tf1"), w=("tf1",))
                aop(lambda: nc.scalar.activation(out=W2[:, 0:ncols], in_=W2[:, 0:ncols], func=AF.Sigmoid, scale=1.5957691216057308), r=("tf1",), w=("tf1",))
                vop(lambda kc=kc: nc.vector.tensor_tensor(out=z[:, kc, 0:ncols], in0=W1[:, 0:ncols], in1=W2[:, 0:ncols], op=ALU.mult), r=("tf0", "tf1"), w=(f"z{kc}",))
                if p == 0:
                    for r in range(2):
                        pgrp([lambda r=r, q=q: nc.tensor.transpose(psM[r][0:NS, q * 128:(q + 1) * 128], xsn[r][:, q, :], ident[:]) for q in range(4)],
                             r=(f"xsn{r}", "const"), w=(f"psM{r}",))
                        vop(lambda r=r: nc.vector.tensor_copy(out=stg[r][0:NS].rearrange("p a b -> p (a b)"), in_=psM[r][0:NS, :]), r=(f"psM{r}",), w=(f"stg{r}",))
                        sy.dma(S, o_st[r][j].rearrange("n (i jj) p -> n i (jj p)", jj=2)[:, kc * 4:(kc + 1) * 4, :], stg[r][0:NS], reads=(f"stg{r}",), writes=("dram_out",))
            cmul(hre, him, FLRE, FLIM, zlre, zlim, W3[:, 0:128], W4[:, 0:128], "tf2", "tf3", ("tb5", "zl"), (f"hst{j}",))
            jobs = []
            for oc in range(KC):
                jobs.append((w_glu[j, oc * 128:(oc + 1) * 128, :], KC, ("a", oc)))
                jobs.append((w_glu[j, D + oc * 128:D + (oc + 1) * 128, :], KC, ("b", oc)))
            st = {}

            def comp(jx, unit, tag):
                kind, oc = tag
                pi = next_psD()
                dense_mm(unit, KC, lambda k: z[:, k, :], pi, ncols, zkeys)
                if kind == "a":
                    st["a"] = pi
                    return
                pa = st["a"]
                aop(lambda: nc.scalar.activation(out=tf[1][:, 0:ncols], in_=psD[pi][:, 0:ncols], func=AF.Sigmoid), r=(f"psD{pi}",), w=("tf1",))
                vop(lambda: nc.vector.tensor_tensor(out=tf[1][:, 0:ncols], in0=tf[1][:, 0:ncols], in1=psD[pa][:, 0:ncols], op=ALU.mult),
                    r=("tf1", f"psD{pa}"), w=("tf1",))
                gop(lambda: nc.gpsimd.tensor_tensor(out=h[:, oc, 0:ncols], in0=h[:, oc, 0:ncols], in1=tf[1][:, 0:ncols], op=ALU.add),
                    r=("tf1", f"h{oc}"), w=(f"h{oc}",))
            run_jobs(jobs, comp)
            barrier()

        def hgrn_mixer(p, ncols, last_pass):
            barrier()
            cv = Carver([mx[:]])
            T1, T2, T3, T4 = tf
            T5 = cv.f32(532); T6 = cv.f32(532); SG = cv.f32(532); O = cv.f32(532)
            QT = cv.bf16(528); KT = cv.bf16(528)
            VTOK = cv.bf16(512, (4, 128)); KHTOK = cv.bf16(512, (4, 128)); VTOKm = cv.bf16(512, (4, 128))
            m3 = cv.f32(1)
            vop(lambda: nc.vector.memset(m3, 1.0), w=("m3",))
            vop(lambda: nc.vector.memset(m3[64:96, :], 0.0), r=("m3",), w=("m3",))
            ATT = cv.bf16(128); Sst = cv.f32(128); SB = cv.bf16(128); EBL = cv.f32(16)
            SHG = cv.f32(1024, (8, 128))
            QF = cv.f32(NS); QK = cv.f32(NS); FS = cv.f32(NS); KS = cv.f32(NS); VTsb = cv.bf16(NS)
            W = hg_w_in
            jobs = []
            for hd in range(32):
                for qi, nm in enumerate(("q", "f", "iv", "g")):
                    jobs.append((W[qi * D + hd * 128:qi * D + (hd + 1) * 128, :], KC, (nm, hd)))
            DSB = [(0, 0), (0, 512), (1, 0), (1, 512)]

            def head_rest(hd):
                lbc = pc_lb[:, hd:hd + 1]; omlc = pc_oml[:, hd:hd + 1]; gnc = pc_gn[:, hd:hd + 1]
                if p == 0:
                    vop(lambda: nc.vector.memset(Sst, 0.0), w=("S",))
                else:
                    sy.dma(S, Sst, hgS[hd], reads=("hgS",), writes=("S",))
                aop(lambda: nc.scalar.activation(out=SB, in_=Sst, func=AF.Copy), r=("S",), w=("SB",))
                vop(lambda: nc.vector.tensor_tensor_scan(out=T4[:, 0:512], data0=msk[:], data1=T3[:, 0:512], initial=0.0, op0=ALU.mult, op1=ALU.add),
                    r=("tf2", "const"), w=("tf3",))
                if ncols > 512:
                    vop(lambda: nc.vector.tensor_copy(out=T4[:, 512:528], in_=T3[:, 512:528]), r=("tf2",), w=("tf3",))
                aop(lambda: nc.scalar.activation(out=T5[:, 0:ncols], in_=T4[:, 0:ncols], func=AF.Exp), r=("tf3",), w=("T5",))
                vop(lambda: nc.vector.tensor_tensor(out=QT[:, 0:ncols], in0=T1[:, 0:ncols], in1=T5[:, 0:ncols], op=ALU.mult), r=("tf0", "T5"), w=("QT",))
                vop(lambda: nc.vector.tensor_copy(out=EBL, in_=T5[:, 0:512].rearrange("p (c l) -> p c l", l=32)[:, :, 31]), r=("T5",), w=("EBL",))
                if ncols > 512:
                    vop(lambda: nc.vector.tensor_tensor(out=QF, in0=T1[:, 512:528], in1=T5[:, 512:528], op=ALU.mult), r=("tf0", "T5"), w=("QF",))
                    vop(lambda: nc.vector.tensor_tensor(out=QK, in0=T1[:, 512:528], in1=T2[:, 512:528], op=ALU.mult), r=("tf0", "tf1"), w=("QK",))
                    vop(lambda: nc.vector.tensor_copy(out=KS, in_=T2[:, 512:528]), r=("tf1",), w=("KS",))
                    vop(lambda: nc.vector.tensor_scalar(out=FS, in0=T2[:, 512:528], scalar1=-1.0, scalar2=1.0, op0=ALU.mult, op1=ALU.add), r=("tf1",), w=("FS",))
                aop(lambda: nc.scalar.activation(out=T5[:, 0:512], in_=T4[:, 0:512], func=AF.Exp, scale=-1.0), r=("tf3", "QT", "EBL", "QF"), w=("T5",))
                vop(lambda: nc.vector.tensor_tensor(out=KT[:, 0:512], in0=T2[:, 0:512], in1=T5[:, 0:512], op=ALU.mult), r=("tf1", "T5"), w=("KT",))
                b3 = T4[:, 0:512].rearrange("p (c l) -> p c l", l=32)
                vop(lambda: nc.vector.tensor_tensor(out=T5[:, 0:512].rearrange("p (c l) -> p c l", l=32), in0=b3[:, :, 31:32].to_broadcast([128, 16, 32]), in1=b3,
                                                    op=ALU.subtract), r=("tf3", "KT"), w=("T5",))
                aop(lambda: nc.scalar.activation(out=T5[:, 0:512], in_=T5[:, 0:512], func=AF.Exp), r=("T5",), w=("T5",))
                vop(lambda: nc.vector.tensor_tensor(out=T6[:, 0:512], in0=T2[:, 0:512], in1=T5[:, 0:512], op=ALU.mult), r=("tf1", "T5"), w=("T6",))

            st = {}

            def comp(jx, unit, tag):
                nm, hd = tag
                pi = next_psD()
                dense_mm(unit, KC, lambda k: u[:, k, :], pi, ncols, ukeys)
                pk = f"psD{pi}"
                lbc = pc_lb[:, hd:hd + 1]; omlc = pc_oml[:, hd:hd + 1]; gnc = pc_gn[:, hd:hd + 1]
                if nm == "q":
                    aop(lambda: nc.scalar.activation(out=T1[:, 0:ncols], in_=psD[pi][:, 0:ncols], func=AF.Silu), r=(pk,), w=("tf0",))
                elif nm == "f":
                    aop(lambda: nc.scalar.activation(out=T2[:, 0:ncols], in_=psD[pi][:, 0:ncols], func=AF.Sigmoid), r=(pk,), w=("tf1",))
                    vop(lambda: nc.vector.tensor_scalar(out=T2[:, 0:ncols], in0=T2[:, 0:ncols], scalar1=omlc, scalar2=lbc, op0=ALU.mult, op1=ALU.add),
                        r=("tf1", "const"), w=("tf1",))
                    aop(lambda: nc.scalar.activation(out=T3[:, 0:ncols], in_=T2[:, 0:ncols], func=AF.Ln), r=("tf1",), w=("tf2",))
                    vop(lambda: nc.vector.tensor_scalar(out=T2[:, 0:ncols], in0=T2[:, 0:ncols], scalar1=-1.0, scalar2=1.0, op0=ALU.mult, op1=ALU.add),
                        r=("tf1", "tf2"), w=("tf1",))
                    head_rest(hd)
                    for blk in range(4):
                        pgrp([lambda blk=blk: nc.tensor.transpose(psM[0][:, blk * 128:(blk + 1) * 128], T6[:, blk * 128:(blk + 1) * 128], ident[:])],
                             r=("T6", "const"), w=("psM0",), acc=(blk > 0))
                    aop(lambda: nc.scalar.activation(out=KHTOK, in_=psM[0][:, 0:512].rearrange("p (a b) -> p a b", b=128), func=AF.Copy), r=("psM0",), w=("KHTOK",))
                elif nm == "iv":
                    aop(lambda: nc.scalar.activation(out=T3[:, 0:ncols], in_=psD[pi][:, 0:ncols], func=AF.Copy), r=(pk,), w=("tf2",))
                    for blk in range(4):
                        pgrp([lambda blk=blk: nc.tensor.transpose(psM[1][:, blk * 128:(blk + 1) * 128], T3[:, blk * 128:(blk + 1) * 128], ident[:])],
                             r=("tf2", "const"), w=("psM1",), acc=(blk > 0))
                    aop(lambda: nc.scalar.activation(out=VTOK, in_=psM[1][:, 0:512].rearrange("p (a b) -> p a b", b=128), func=AF.Copy), r=("psM1",), w=("VTOK",))
                    vop(lambda: nc.vector.tensor_scalar(out=VTOKm, in0=VTOK, scalar1=m3[:, 0:1], scalar2=None, op0=ALU.mult),
                        r=("VTOK", "m3"), w=("VTOKm",))
                    if ncols > 512:
                        vop(lambda: nc.vector.tensor_copy(out=VTsb, in_=T3[:, 512:528]), r=("tf2",), w=("VTsb",))
                else:
                    aop(lambda: nc.scalar.activation(out=SG[:, 0:ncols], in_=psD[pi][:, 0:ncols], func=AF.Sigmoid), r=(pk,), w=("SG",))
                    for blk in range(4):
                        cb = slice(blk * 128, (blk + 1) * 128)
                        pgrp([lambda: nc.tensor.matmul(psM[0][:, 0:128], KT[:, cb], QT[:, cb], start=True, stop=True)], r=("KT", "QT"), w=("psM0",))
                        vop(lambda: nc.vector.tensor_tensor(out=ATT, in0=psM[0][:, 0:128], in1=cmask[:], op=ALU.mult), r=("psM0", "const"), w=("ATT",))
                        fns = []
                        for c in range(4):
                            di, co = DSB[c]
                            if c < 3:
                                fns.append(lambda c=c, di=di, co=co: nc.tensor.matmul(psD[di][:, co:co + 128], KHTOK[c * 32:(c + 1) * 32, blk, :],
                                                                                      VTOK[c * 32:(c + 1) * 32, blk, :], start=True, stop=True))
                            else:
                                fns.append(lambda di=di, co=co: nc.tensor.matmul(psD[di][:, co:co + 128], KHTOK[64:128, blk, :],
                                                                                 VTOKm[64:128, blk, :], start=True, stop=True))
                        pgrp(fns, r=("KHTOK", "VTOK", "VTOKm"), w=("psD0", "psD1"))
                        pgrp([lambda: nc.tensor.matmul(psD[2][:, cb], VTOK[:, blk, :], ATT, start=True, stop=False, skip_group_check=True)],
                             r=("VTOK", "ATT"), w=("psD2",), acc=(blk > 0))
                        for c in range(4):
                            cg = blk * 4 + c
                            cc_ = slice(blk * 128 + c * 32, blk * 128 + (c + 1) * 32)
                            di, co = DSB[c]
                            pgrp([lambda cc_=cc_: nc.tensor.matmul(psD[2][:, cc_], SB, QT[:, cc_], start=False, stop=True, skip_group_check=True)],
                                 r=("SB", "QT"), w=("psD2",), acc=True)
                            vop(lambda cg=cg, di=di, co=co: nc.vector.scalar_tensor_tensor(out=Sst, in0=Sst, scalar=EBL[:, cg:cg + 1], in1=psD[di][:, co:co + 128],
                                                                                         op0=ALU.mult, op1=ALU.add), r=("S", "EBL", f"psD{di}"), w=("S",))
                            aop(lambda: nc.scalar.activation(out=SB, in_=Sst, func=AF.Copy), r=("S",), w=("SB",))
                    aop(lambda: nc.scalar.activation(out=O[:, 0:512], in_=psD[2][:, 0:512], func=AF.Copy), r=("psD2",), w=("O",))
                    if last_pass:
                        sy.dma(S, o_phg[hd], Sst, reads=("S",), writes=("dram_out",))
                    else:
                        sy.dma(S, hgS[hd], Sst, reads=("S",), writes=("hgS",))
                    if ncols > 512:
                        pgrp([lambda: nc.tensor.matmul(psM[0][:, 128:128 + NS], onesf[:], QK, start=True, stop=True)], r=("QK", "const"), w=("psM0",))
                        vop(lambda: nc.vector.tensor_tensor(out=O[:, 512:528], in0=psM[0][:, 128:128 + NS], in1=T3[:, 512:528], op=ALU.mult),
                            r=("psM0", "tf2"), w=("O",))
                        for half in range(2):
                            n0 = half * 8
                            sy.dma(S, SHG, i_hg[n0:n0 + 8, hd].rearrange("n k v -> k n v"), writes=("SHG",))
                            for nl in range(8):
                                n = n0 + nl
                                pgrp([lambda n=n, nl=nl: nc.tensor.matmul(psM[1][:, n:n + 1], SHG[:, nl, :], QF[:, n:n + 1], start=True, stop=True)],
                                     r=("SHG", "QF"), w=("psM1",), acc=(n > 0))
                                pgrp([lambda n=n: nc.tensor.matmul(psM[0][:, 256:384], VTsb[:, n:n + 1].to_broadcast([128, 128]), identb[:], start=True, stop=True)],
                                     r=("VTsb", "const"), w=("psM0",))
                                vop(lambda n=n, nl=nl: nc.vector.tensor_scalar(out=SHG[:, nl, :], in0=SHG[:, nl, :], scalar1=FS[:, n:n + 1], scalar2=None, op0=ALU.mult),
                                    r=("SHG", "FS", "psM1"), w=("SHG",))
                                vop(lambda n=n, nl=nl: nc.vector.scalar_tensor_tensor(out=SHG[:, nl, :], in0=psM[0][:, 256:384], scalar=KS[:, n:n + 1], in1=SHG[:, nl, :],
                                                                                     op0=ALU.mult, op1=ALU.add), r=("SHG", "KS", "psM0"), w=("SHG",))
                            sy.dma(S, o_shg[n0:n0 + 8, hd].rearrange("n k v -> k n v"), SHG, reads=("SHG",), writes=("dram_out",))
                        vop(lambda: nc.vector.tensor_tensor(out=O[:, 512:528], in0=O[:, 512:528], in1=psM[1][:, 0:NS], op=ALU.add), r=("O", "psM1"), w=("O",))
                    aop(lambda: nc.scalar.activation(out=T5[:, 0:ncols], in_=O[:, 0:ncols], func=AF.Square), r=("O",), w=("T5",))
                    pgrp([lambda: nc.tensor.matmul(psM[0][:, 0:512], onesf[:], T5[:, 0:512], start=True, stop=True)], r=("T5", "const"), w=("psM0",))
                    aop(lambda: nc.scalar.activation(out=T6[:, 0:512], in_=psM[0][:, 0:512], func=AF.Sqrt, bias=epsc[:, 0:1], scale=1.0 / 128), r=("psM0", "const"), w=("T6",))
                    if ncols > 512:
                        pgrp([lambda: nc.tensor.matmul(psM[1][:, 32:32 + NS], onesf[:], T5[:, 512:528], start=True, stop=True)], r=("T5", "const"), w=("psM1",))
                        aop(lambda: nc.scalar.activation(out=T6[:, 512:528], in_=psM[1][:, 32:32 + NS], func=AF.Sqrt, bias=epsc[:, 0:1], scale=1.0 / 128),
                            r=("psM1", "const"), w=("T6",))
                    vop(lambda: nc.vector.reciprocal(out=T6[:, 0:ncols], in_=T6[:, 0:ncols]), r=("T6",), w=("T6",))
                    vop(lambda: nc.vector.scalar_tensor_tensor(out=O[:, 0:ncols], in0=O[:, 0:ncols], scalar=gnc, in1=T6[:, 0:ncols], op0=ALU.mult, op1=ALU.mult),
                        r=("O", "T6", "const"), w=("O",))
                    vop(lambda: nc.vector.tensor_tensor(out=z[:, hd, 0:ncols], in0=O[:, 0:ncols], in1=SG[:, 0:ncols], op=ALU.mult), r=("O", "SG"), w=(f"z{hd}",))
            run_jobs(jobs, comp)
            out_proj_residual(hg_w_out, ncols)
            barrier()

        gop(lambda: nc.gpsimd.memset(ident[:], 0.0), w=("const",))
        gop(lambda: nc.gpsimd.affine_select(out=ident[:], in_=ident[:], compare_op=ALU.not_equal, fill=1.0, base=0,
                                            pattern=[[-1, 128]], channel_multiplier=1), r=("const",), w=("const",))
        vop(lambda: nc.vector.tensor_copy(out=identb[:], in_=ident[:]), r=("const",), w=("const",))
        gop(lambda: nc.gpsimd.memset(ones_b[:], 1.0), w=("const",))
        gop(lambda: nc.gpsimd.memset(onesf[:], 1.0), w=("const",))
        gop(lambda: nc.gpsimd.memset(epsc[:], EPS), w=("const",))
        gop(lambda: nc.gpsimd.memset(ccarry[:], 0.0), w=("ccarry",))
        gop(lambda: nc.gpsimd.memset(msk[:], 1.0), w=("const",))
        gop(lambda: nc.gpsimd.memset(msk[:].rearrange("p (c l) -> p c l", l=32)[:, :, 0:1], 0.0), r=("const",), w=("const",))
        gop(lambda: nc.gpsimd.memset(cmask[:], 1.0), w=("const",))
        cm3 = cmask[:].rearrange("p (a b) -> p a b", b=32)
        gop(lambda: nc.gpsimd.affine_select(out=cm3, in_=cm3, compare_op=ALU.is_ge, fill=0.0, base=0,
                                            pattern=[[-32, 4], [0, 32]], channel_multiplier=1), r=("const",), w=("const",))
        gop(lambda: nc.gpsimd.affine_select(out=cm3, in_=cm3, compare_op=ALU.is_ge, fill=0.0, base=0,
                                            pattern=[[32, 4], [1, 32]], channel_multiplier=-1), r=("const",), w=("const",))
        gop(lambda: nc.gpsimd.affine_select(out=cm3, in_=cm3, compare_op=ALU.is_ge, fill=0.0, base=31,
                                            pattern=[[32, 4], [0, 32]], channel_multiplier=-1), r=("const",), w=("const",))
        load_cols(pc_nmix, norm_mix.rearrange("l (k p) -> (l k) p", p=128), 128)
        load_cols(pc_nffn, norm_ffn.rearrange("l (k p) -> (l k) p", p=128), 128)
        load_cols(pc_nfin, norm_final.rearrange("l (k p) -> (l k) p", p=128), 32)
        load_cols(pc_convw, conv_w.rearrange("l (k p) -> (l k) p", p=128), 96)
        load_cols(pc_s5d, s5_d.rearrange("l (k p) -> (l k) p", p=128), 64)
        load_cols(pc_gn, hg_gn.rearrange("l (k p) -> (l k) p", p=128), 32)
        lg = tf[0][:, 0:128]
        load_cols(lg, hg_lb.rearrange("l (k p) -> (l k) p", p=128), 128)
        mxl = tf[1][:, 0:32]
        vop(lambda: nc.vector.tensor_tensor(out=mxl, in0=lg[:, 0:32], in1=lg[:, 32:64], op=ALU.max), r=("const",), w=("const",))
        vop(lambda: nc.vector.tensor_tensor(out=mxl, in0=mxl, in1=lg[:, 64:96], op=ALU.max), r=("const",), w=("const",))
        vop(lambda: nc.vector.tensor_tensor(out=mxl, in0=mxl, in1=lg[:, 96:128], op=ALU.max), r=("const",), w=("const",))
        for l4 in range(4):
            vop(lambda l4=l4: nc.vector.tensor_tensor(out=lg[:, l4 * 32:(l4 + 1) * 32], in0=lg[:, l4 * 32:(l4 + 1) * 32], in1=mxl, op=ALU.subtract), r=("const",), w=("const",))
        aop(lambda: nc.scalar.activation(out=lg, in_=lg, func=AF.Exp), r=("const",), w=("const",))
        sm = tf[1][:, 32:64]; nm12 = tf[1][:, 64:96]
        vop(lambda: nc.vector.tensor_tensor(out=nm12, in0=lg[:, 32:64], in1=lg[:, 64:96], op=ALU.add), r=("const",), w=("const",))
        vop(lambda: nc.vector.tensor_tensor(out=sm, in0=lg[:, 0:32], in1=lg[:, 96:128], op=ALU.add), r=("const",), w=("const",))
        vop(lambda: nc.vector.tensor_tensor(out=sm, in0=sm, in1=nm12, op=ALU.add), r=("const",), w=("const",))
        vop(lambda: nc.vector.reciprocal(out=sm, in_=sm), r=("const",), w=("const",))
        vop(lambda: nc.vector.tensor_tensor(out=pc_lb[:], in0=nm12, in1=sm, op=ALU.mult), r=("const",), w=("const",))
        vop(lambda: nc.vector.tensor_scalar(out=pc_oml[:], in0=pc_lb[:], scalar1=-1.0, scalar2=1.0, op0=ALU.mult, op1=ALU.add), r=("const",), w=("const",))
        if 0 in layers:
            s5_setup(0)
        if 3 in layers:
            s5_setup(1)

        for p in range(npass):
            ncols = 528 if p == 0 else 512
            for tt in range(4):
                for half in range(2):
                    r0 = p * TP + tt * 128
                    sy.dma(S, ost_flat, xp[r0:r0 + 128, half * 2048:(half + 1) * 2048], writes=TFK)
                    for g4 in range(4):
                        pm = g4 % 2
                        fns = []
                        for q4 in range(4):
                            fns.append(lambda g4=g4, q4=q4, pm=pm: nc.tensor.transpose(psM[pm][:, q4 * 128:(q4 + 1) * 128], ostage[:, g4, q4 * 128:(q4 + 1) * 128], ident[:]))
                        pgrp(fns, r=TFK + ("const",), w=(f"psM{pm}",))
                        kc0 = half * 16 + g4 * 4
                        aop(lambda kc0=kc0, pm=pm, tt=tt: nc.scalar.activation(out=h[:, kc0:kc0 + 4, tt * 128:(tt + 1) * 128],
                                                                               in_=psM[pm][:, :].rearrange("p (a b) -> p a b", b=128), func=AF.Copy),
                            r=(f"psM{pm}",), w=tuple(f"h{kc0 + i}" for i in range(4)))
            if p == 0:
                for half in range(2):
                    sy.dma(S, ost_flat[0:NS, :], xs[:, half * 2048:(half + 1) * 2048], writes=TFK)
                    fns = []
                    for q in range(16):
                        fns.append(lambda q=q: nc.tensor.transpose(psM[1][:, q * NS:(q + 1) * NS], ost_flat[0:NS, q * 128:(q + 1) * 128], ident[0:NS, 0:NS]))
                    pgrp(fns, r=TFK + ("const",), w=("psM1",))
                    vop(lambda half=half: nc.vector.tensor_copy(out=h[:, half * 16:(half + 1) * 16, 512:528],
                                                                in_=psM[1][:, 0:16 * NS].rearrange("p (a b) -> p a b", b=NS)),
                        r=("psM1",), w=tuple(f"h{half * 16 + i}" for i in range(16)))

            for layer in range(4):
                kind = layer % 3
                rmsnorm(lambda kc, layer=layer: pc_nmix[:, layer * 32 + kc:layer * 32 + kc + 1], ncols)
                if layer in layers:
                    if kind == 0:
                        s5_mixer(layer // 3, layer, p, ncols)
                    elif kind == 1:
                        conv_mixer(p, ncols)
                    else:
                        hgrn_mixer(p, ncols, p == npass - 1)
                if do_ffn:
                    rmsnorm(lambda kc, layer=layer: pc_nffn[:, layer * 32 + kc:layer * 32 + kc + 1], ncols)
                    ffn(layer, ncols)

            rmsnorm(lambda kc: pc_nfin[:, kc:kc + 1], ncols, make_u=False)
            for kc in range(KC):
                vop(lambda kc=kc: nc.vector.scalar_tensor_tensor(out=h[:, kc, 0:ncols], in0=h[:, kc, 0:ncols], scalar=pc_nfin[:, kc:kc + 1],
                                                                 in1=rstd[:, 0:ncols], op0=ALU.mult, op1=ALU.mult),
                    r=(f"h{kc}", "rstd", "const"), w=(f"h{kc}",))
            for tt in range(4):
                r0 = p * TP + tt * 128
                transpose_cols_out(lambda kc, tt=tt: h[:, kc, tt * 128:(tt + 1) * 128], 128, yp[r0:r0 + 128, :], hkeys)
            if p == 0:
                transpose_cols_out(lambda kc: h[:, kc, 512:528], NS, ys[:, :], hkeys)

        pgrp([lambda: nc.tensor.transpose(psM[0][0:64, 0:128], ccarry[:].rearrange("p a b -> p (a b)"), ident[:])], r=("ccarry", "const"), w=("psM0",))
        vop(lambda: nc.vector.tensor_copy(out=pstage[0:64, :], in_=psM[0][0:64, 0:128]), r=("psM0",), w=("pstage",))
        sy.dma(S, o_pconv.rearrange("r (k p) -> (r k) p", p=128), pstage[0:64, :], reads=("pstage",), writes=("dram_out",))
        for j in range(2):
            for r, od in ((0, o_pre), (1, o_pim)):
                pgrp([lambda j=j, r=r: nc.tensor.transpose(psM[0][:, 0:128], hst[j][r][:], ident[:])], r=(f"hst{j}", "const"), w=("psM0",))
                vop(lambda: nc.vector.tensor_copy(out=pstage[:, :], in_=psM[0][:, 0:128]), r=("psM0",), w=("pstage",))
                sy.dma(S, od[j].rearrange("(i jj) p -> i (jj p)", jj=2), pstage[:, :], reads=("pstage",), writes=("dram_out",))

        barrier()
    return nc


_CACHE = {}


def kernel(**inp):
    f32 = np.float32
    nc = _CACHE.get("nc")
    if nc is None:
        nc = build()
        _CACHE["nc"] = nc
    def tile_in(W):
        K_, N_ = W.shape
        return np.ascontiguousarray(W.reshape(K_ // 128, 128, N_ // 128, 128).transpose(2, 1, 0, 3)).reshape(N_, K_)

    shared = {}
    for k in ("norm_mix", "norm_ffn", "s5_a_re", "s5_a_im", "s5_log_dt", "s5_b_re", "s5_b_im", "s5_c_re", "s5_c_im",
              "s5_d", "hgrn_lb_logits", "hgrn_gnorm"):
        shared[k] = np.ascontiguousarray(inp[k], dtype=f32)
    shared["norm_final"] = np.ascontiguousarray(inp["norm_final"], dtype=f32).reshape(1, D)
    shared["conv_w"] = np.ascontiguousarray(inp["conv_w"], dtype=f32).reshape(3, D)
    shared["s5_w_glu_t"] = np.stack([tile_in(np.asarray(inp["s5_w_glu"][j], dtype=f32)) for j in range(2)], 0)
    shared["conv_w_in_t"] = tile_in(np.asarray(inp["conv_w_in"][0], dtype=f32))
    shared["conv_w_out_t"] = tile_in(np.asarray(inp["conv_w_out"][0], dtype=f32))
    shared["hgrn_w_in_t"] = tile_in(np.asarray(inp["hgrn_w_in"][0], dtype=f32))
    shared["hgrn_w_out_t"] = tile_in(np.asarray(inp["hgrn_w_out"][0], dtype=f32))
    shared["ffn_w_in_t"] = np.stack([tile_in(np.asarray(inp["ffn_w_in"][l], dtype=f32)) for l in range(4)], 0)
    wo = np.asarray(inp["ffn_w_out"], dtype=f32)
    shared["ffn_w_out_ta"] = np.stack([np.stack([tile_in(wo[l, b * D:(b + 1) * D]) for b in range(2)], 0) for l in range(4)], 0)
    shared["ffn_w_out_tb"] = np.stack([tile_in(wo[l, 2 * D:]) for l in range(4)], 0)
    in_maps = []
    for c in range(8):
        m = dict(shared)
        m["xp"] = np.ascontiguousarray(inp["x_prompt"][c % 4], dtype=f32)
        sl = slice(c * NS, (c + 1) * NS)
        m["xs"] = np.ascontiguousarray(inp["x_sample"][sl, 0, :], dtype=f32)
        m["i_s5re"] = np.ascontiguousarray(inp["state_s5_re"][:, sl], dtype=f32)
        m["i_s5im"] = np.ascontiguousarray(inp["state_s5_im"][:, sl], dtype=f32)
        m["i_conv"] = np.ascontiguousarray(inp["state_conv"][0, sl], dtype=f32)
        m["i_hg"] = np.ascontiguousarray(inp["state_hgrn"][0, sl], dtype=f32)
        in_maps.append(m)
    res = run_bass_kernel_spmd(nc, in_maps, core_ids=list(range(8)))
    R = res.results
    y_prompt = np.stack([R[c]["yp"] for c in range(4)], 0)
    y_sample = np.concatenate([R[c]["ys"] for c in range(8)], 0)[:, None, :]
    p_re = np.stack([R[c]["o_pre"] for c in range(4)], 1)
    p_im = np.stack([R[c]["o_pim"] for c in range(4)], 1)
    p_conv = np.stack([R[c]["o_pconv"] for c in range(4)], 0)[None]
    p_hg = np.stack([R[c]["o_phg"] for c in range(4)], 0)[None]
    s_re = np.concatenate([R[c]["o_sre"] for c in range(8)], 1)
    s_im = np.concatenate([R[c]["o_sim"] for c in range(8)], 1)
    s_conv = np.concatenate([R[c]["o_sconv"] for c in range(8)], 0)[None]
    s_hg = np.concatenate([R[c]["o_shg"] for c in range(8)], 0)[None]
    return (y_prompt.astype(f32), y_sample.astype(f32), p_re.astype(f32), p_im.astype(f32), p_conv.astype(f32),
            p_hg.astype(f32), s_re.astype(f32), s_im.astype(f32), s_conv.astype(f32), s_hg.astype(f32))
```

```python
import contextlib
import math
import numpy as np
import concourse.bass as bass
import concourse.mybir as mybir
from concourse.bass_utils import run_bass_kernel_spmd

F32 = mybir.dt.float32
BF16 = mybir.dt.bfloat16
I32 = mybir.dt.int32
AF = mybir.ActivationFunctionType
ALU = mybir.AluOpType

D = 4096
KC = 32
DFF = 11008
NFC = 86
TP = 512
NS = 16
SEQ = 2048
EPS = 1e-6
NWB = 4


class Q:
    def __init__(self, nc, es, name, handle, nslots=0):
        self.h = handle
        self.name = name
        self.sem = es.enter_context(nc.semaphore("q_" + name))
        self.cnt = 0
        self.seen = {}
        self.slots = [[es.enter_context(nc.semaphore(f"d_{name}{i}")), 0] for i in range(nslots)]
        self.si = 0


class Sync:
    def __init__(self):
        self.lw = {}
        self.lr = {}

    def deps(self, q, reads, writes, acc=False):
        need = {}

        def add(t):
            if t is None:
                return
            s, v = t
            k = id(s)
            if k not in need or need[k][1] < v:
                need[k] = (s, v)
        for k in reads:
            add(self.lw.get(k))
        for k in writes:
            t = self.lw.get(k)
            if not (acc and t is not None and t[0] is q.sem):
                add(t)
            for t in self.lr.get(k, {}).values():
                add(t)
        for k, (s, v) in need.items():
            if q.seen.get(k, 0) < v:
                q.h.wait_ge(s, v)
                q.seen[k] = v

    def commit(self, tok, reads, writes):
        for k in reads:
            self.lr.setdefault(k, {})[id(tok[0])] = tok
        for k in writes:
            self.lw[k] = tok
            self.lr[k] = {}

    def op(self, q, fn, reads=(), writes=(), acc=False):
        self.deps(q, reads, writes, acc)
        inst = fn()
        q.cnt += 1
        inst.then_inc(q.sem, 1)
        tok = (q.sem, q.cnt)
        self.commit(tok, reads, writes)
        return tok

    def group(self, q, fns, reads=(), writes=(), acc=False):
        self.deps(q, reads, writes, acc)
        inst = None
        for f in fns:
            inst = f()
        q.cnt += 1
        inst.then_inc(q.sem, 1)
        tok = (q.sem, q.cnt)
        self.commit(tok, reads, writes)
        return tok

    def dma(self, q, out, in_, reads=(), writes=(), **kw):
        slot = q.slots[q.si % len(q.slots)]
        q.si += 1
        sem, prev = slot
        if prev > 0 and q.seen.get(id(sem), 0) < prev:
            q.h.wait_ge(sem, prev)
            q.seen[id(sem)] = prev
        self.deps(q, reads, writes)
        inst = q.h.dma_start(out=out, in_=in_, **kw)
        inst.then_inc(sem, 16)
        slot[1] = prev + 16
        tok = (sem, prev + 16)
        self.commit(tok, reads, writes)
        return tok


PI = math.pi


class Carver:
    def __init__(self, regions):
        self.regions = regions
        self.ri = 0
        self.off = 0

    def f32(self, n, shape=None):
        while self.off + n > self.regions[self.ri].shape[1]:
            self.ri += 1
            self.off = 0
        ap = self.regions[self.ri][:, self.off:self.off + n]
        self.off += n
        if shape is not None:
            names = " ".join(f"d{i}" for i in range(len(shape)))
            kw = {f"d{i}": s for i, s in enumerate(shape)}
            ap = ap.rearrange(f"p ({names}) -> p {names}", **kw)
        return ap

    def mark(self):
        return (self.ri, self.off)

    def reset(self, m):
        self.ri, self.off = m

    def bf16(self, n, shape=None):
        assert n % 2 == 0
        ap = self.f32(n // 2).bitcast(BF16)
        if shape is not None:
            names = " ".join(f"d{i}" for i in range(len(shape)))
            kw = {f"d{i}": s for i, s in enumerate(shape)}
            ap = ap.rearrange(f"p ({names}) -> p {names}", **kw)
        return ap


def build(npass=4, layers=(0, 1, 2, 3), do_ffn=True):
    nc = bass.Bass("TRN2", target_bir_lowering=False)

    def din(name, shape):
        return nc.dram_tensor(name, list(shape), F32, kind="ExternalInput").ap()

    def dout(name, shape):
        return nc.dram_tensor(name, list(shape), F32, kind="ExternalOutput").ap()

    xp = din("xp", [SEQ, D]); xs = din("xs", [NS, D])
    i_s5re = din("i_s5re", [2, NS, 256, 64]); i_s5im = din("i_s5im", [2, NS, 256, 64])
    i_conv = din("i_conv", [NS, 2, D]); i_hg = din("i_hg", [NS, 32, 128, 128])
    norm_mix = din("norm_mix", [4, D]); norm_ffn = din("norm_ffn", [4, D]); norm_final = din("norm_final", [1, D])
    a_re = din("s5_a_re", [2, 256, 64]); a_im = din("s5_a_im", [2, 256, 64]); log_dt = din("s5_log_dt", [2, 256])
    b_re = din("s5_b_re", [2, 256, 64, 16]); b_im = din("s5_b_im", [2, 256, 64, 16])
    c_re = din("s5_c_re", [2, 256, 16, 64]); c_im = din("s5_c_im", [2, 256, 16, 64])
    s5_d = din("s5_d", [2, D]); w_glu = din("s5_w_glu_t", [2, 2 * D, D])
    conv_w_in = din("conv_w_in_t", [3 * D, D]); conv_w = din("conv_w", [3, D]); conv_w_out = din("conv_w_out_t", [D, D])
    hg_w_in = din("hgrn_w_in_t", [4 * D, D]); hg_lb = din("hgrn_lb_logits", [4, D]); hg_gn = din("hgrn_gnorm", [1, D])
    hg_w_out = din("hgrn_w_out_t", [D, D])
    ffn_w_in = din("ffn_w_in_t", [4, 2 * DFF, D]); ffn_woA = din("ffn_w_out_ta", [4, 2, D, D]); ffn_woB = din("ffn_w_out_tb", [4, D, 22 * 128])

    yp = dout("yp", [SEQ, D]); ys = dout("ys", [NS, D])
    o_pre = dout("o_pre", [2, 256, 64]); o_pim = dout("o_pim", [2, 256, 64])
    o_pconv = dout("o_pconv", [2, D]); o_phg = dout("o_phg", [32, 128, 128])
    o_sre = dout("o_sre", [2, NS, 256, 64]); o_sim = dout("o_sim", [2, NS, 256, 64])
    o_sconv = dout("o_sconv", [NS, 2, D]); o_shg = dout("o_shg", [NS, 32, 128, 128])

    s5w = nc.dram_tensor("s5w", [2, 4, 128, KC * 128], BF16, kind="Internal").ap()
    s5t = nc.dram_tensor("s5t", [2, 128, 5 * 128], F32, kind="Internal").ap()
    s5tab = nc.dram_tensor("s5tab", [2, 128, 128, 4 * 512], F32, kind="Internal").ap()
    hgS = nc.dram_tensor("hgS", [32, 128, 128], F32, kind="Internal").ap()

    es = contextlib.ExitStack()
    with es:
        def sb(name, shape, dt=F32):
            return es.enter_context(nc.sbuf_tensor(name, list(shape), dt))

        def pst(name, shape, dt=F32):
            return es.enter_context(nc.psum_tensor(name, list(shape), dt))

        h = sb("h", [128, KC, 528])
        u = sb("u", [128, KC, 528], BF16)
        z = sb("z", [128, KC, 528], BF16)
        wb = [sb(f"wb{i}", [128, KC, 128], BF16) for i in range(NWB)]
        scr = sb("scr", [128, 4 * 532])
        tf = [scr[:, i * 532:(i + 1) * 532] for i in range(4)]
        ostage = scr[:, 0:2048].rearrange("p (a b) -> p a b", b=512)
        ost_flat = scr[:, 0:2048]
        TFK = ("tf0", "tf1", "tf2", "tf3")
        MXN = 5600
        mx = sb("mx", [128, MXN])
        sq = [sb(f"sq{i}", [128, 528], BF16) for i in range(2)]
        rstd = sb("rstd", [128, 528])
        ident = sb("ident", [128, 128])
        identb = sb("identb", [128, 128], BF16)
        ones_b = sb("ones_b", [128, 128], BF16)
        onesf = sb("onesf", [128, 128])
        pstage = sb("pstage", [128, 128])
        pc_nmix = sb("pc_nmix", [128, 128]); pc_nffn = sb("pc_nffn", [128, 128]); pc_nfin = sb("pc_nfin", [128, 32])
        pc_convw = sb("pc_convw", [128, 96]); pc_s5d = sb("pc_s5d", [128, 64]); pc_gn = sb("pc_gn", [128, 32])
        pc_lb = sb("pc_lb", [128, 32]); pc_oml = sb("pc_oml", [128, 32])
        ccarry = sb("ccarry", [128, 2, KC])
        hst = [[sb(f"hst{j}{r}", [128, 128]) for r in range(2)] for j in range(2)]
        epsc = sb("epsc", [128, 1])
        msk = sb("msk", [128, 512])
        cmask = sb("cmask", [128, 128])

        psD = [pst(f"psD{i}", [128, 1024]) for i in range(3)]
        psM = [pst(f"psM{i}", [128, 512]) for i in range(2)]

        V = Q(nc, es, "v", nc.vector)
        A = Q(nc, es, "a", nc.scalar)
        G = Q(nc, es, "g", nc.gpsimd, nslots=8)
        P = Q(nc, es, "p", nc.tensor)
        S = Q(nc, es, "s", nc.sync, nslots=8)
        QS = (V, A, G, P, S)
        sy = Sync()
        es.enter_context(nc.Block())

        def barrier():
            for q in QS:
                for o in QS:
                    if o is not q and o.cnt > q.seen.get(id(o.sem), 0):
                        q.h.wait_ge(o.sem, o.cnt)
                        q.seen[id(o.sem)] = o.cnt
                for o in (G, S):
                    for sem, val in o.slots:
                        if val > q.seen.get(id(sem), 0):
                            q.h.wait_ge(sem, val)
                            q.seen[id(sem)] = val

        def vop(fn, r=(), w=(), acc=False):
            return sy.op(V, fn, reads=r, writes=w, acc=acc)

        def aop(fn, r=(), w=()):
            return sy.op(A, fn, reads=r, writes=w)

        def gop(fn, r=(), w=()):
            return sy.op(G, fn, reads=r, writes=w)

        def pgrp(fns, r=(), w=(), acc=False):
            return sy.group(P, fns, reads=r, writes=w, acc=acc)

        wstate = {"n": 0}

        def wload(dram_rows_ap, nk):
            b = wstate["n"] % NWB
            wstate["n"] += 1
            sy.dma(G, wb[b][:].rearrange("p a b -> p (a b)")[:, 0:nk * 128], dram_rows_ap, reads=(), writes=(f"wb{b}",))
            return b

        def run_jobs(jobs, compute):
            issued = []
            for j in range(len(jobs)):
                while len(issued) < min(len(jobs), j + NWB - 1):
                    ap, nk, _ = jobs[len(issued)]
                    issued.append(wload(ap, nk))
                compute(j, issued[j], jobs[j][2])

        dps = {"n": 0}

        def next_psD():
            i = dps["n"] % 3
            dps["n"] += 1
            return i

        def dense_mm(unit, nk, rhs_fn, ps_i, ncols, rkeys):
            fns = []
            for k in range(nk):
                fns.append(lambda k=k: nc.tensor.matmul(psD[ps_i][:, 0:512], wb[unit][:, k, :], rhs_fn(k)[:, 0:512],
                                                        start=(k == 0), stop=(k == nk - 1)))
            if ncols > 512:
                for k in range(nk):
                    fns.append(lambda k=k: nc.tensor.matmul(psD[ps_i][:, 512:ncols], wb[unit][:, k, :], rhs_fn(k)[:, 512:ncols],
                                                            start=(k == 0), stop=(k == nk - 1)))
            pgrp(fns, r=tuple(rkeys) + (f"wb{unit}",), w=(f"psD{ps_i}",))

        def rmsnorm(wcol, ncols, make_u=True):
            for kc in range(KC):
                s_ = sq[kc % 2]
                aop(lambda kc=kc, s_=s_: nc.scalar.activation(out=s_[:, 0:ncols], in_=h[:, kc, 0:ncols], func=AF.Square),
                    r=(f"h{kc}",), w=(f"sq{kc % 2}",))
                pgrp([lambda kc=kc, s_=s_: nc.tensor.matmul(psM[0][:, 0:512], ones_b[:], s_[:, 0:512], start=(kc == 0), stop=(kc == KC - 1))],
                     r=(f"sq{kc % 2}", "const"), w=("psM0",), acc=(kc > 0))
                if ncols > 512:
                    pgrp([lambda kc=kc, s_=s_: nc.tensor.matmul(psM[1][:, 0:NS], ones_b[:], s_[:, 512:ncols], start=(kc == 0), stop=(kc == KC - 1))],
                         r=(f"sq{kc % 2}", "const"), w=("psM1",), acc=(kc > 0))
            aop(lambda: nc.scalar.activation(out=tf[0][:, 0:512], in_=psM[0][:, 0:512], func=AF.Sqrt, bias=epsc[:, 0:1], scale=1.0 / D),
                r=("psM0", "const"), w=("tf0",))
            if ncols > 512:
                aop(lambda: nc.scalar.activation(out=tf[0][:, 512:ncols], in_=psM[1][:, 0:NS], func=AF.Sqrt, bias=epsc[:, 0:1], scale=1.0 / D),
                    r=("psM1", "const"), w=("tf0",))
            vop(lambda: nc.vector.reciprocal(out=rstd[:, 0:ncols], in_=tf[0][:, 0:ncols]), r=("tf0",), w=("rstd",))
            if not make_u:
                return
            for kc in range(KC):
                vop(lambda kc=kc: nc.vector.scalar_tensor_tensor(out=u[:, kc, 0:ncols], in0=h[:, kc, 0:ncols], scalar=wcol(kc),
                                                                 in1=rstd[:, 0:ncols], op0=ALU.mult, op1=ALU.mult),
                    r=(f"h{kc}", "rstd", "const"), w=(f"u{kc}",))

        ukeys = tuple(f"u{k}" for k in range(KC))
        zkeys = tuple(f"z{k}" for k in range(KC))
        hkeys = tuple(f"h{k}" for k in range(KC))

        def ffn(layer, ncols):
            for blk in range(3):
                fc0 = blk * 32
                nf = min(NFC, fc0 + 32) - fc0
                jobs = []
                for fl in range(nf):
                    fc = fc0 + fl
                    jobs.append((ffn_w_in[layer, fc * 128:(fc + 1) * 128, :], KC, ("g", fl)))
                    jobs.append((ffn_w_in[layer, DFF + fc * 128:DFF + (fc + 1) * 128, :], KC, ("u", fl)))
                st = {}

                def comp_in(j, unit, tag):
                    kind, fl = tag
                    pi = next_psD()
                    dense_mm(unit, KC, lambda k: u[:, k, :], pi, ncols, ukeys)
                    if kind == "g":
                        st["g"] = pi
                    else:
                        pg = st["g"]
                        aop(lambda: nc.scalar.activation(out=tf[1][:, 0:ncols], in_=psD[pg][:, 0:ncols], func=AF.Silu),
                            r=(f"psD{pg}",), w=("tf1",))
                        vop(lambda: nc.vector.tensor_tensor(out=z[:, fl, 0:ncols], in0=tf[1][:, 0:ncols], in1=psD[pi][:, 0:ncols], op=ALU.mult),
                            r=("tf1", f"psD{pi}"), w=(f"z{fl}",))
                run_jobs(jobs, comp_in)
                if blk < 2:
                    jobs = [(ffn_woA[layer, blk, oc * 128:(oc + 1) * 128, :], nf, oc) for oc in range(KC)]
                else:
                    jobs = [(ffn_woB[layer, oc * 128:(oc + 1) * 128, :], nf, oc) for oc in range(KC)]

                def comp_out(j, unit, oc):
                    pi = next_psD()
                    dense_mm(unit, nf, lambda k: z[:, k, :], pi, ncols, zkeys[:nf])
                    vop(lambda: nc.vector.tensor_tensor(out=h[:, oc, 0:ncols], in0=h[:, oc, 0:ncols], in1=psD[pi][:, 0:ncols], op=ALU.add),
                        r=(f"psD{pi}", f"h{oc}"), w=(f"h{oc}",))
                run_jobs(jobs, comp_out)

        def out_proj_residual(wmat, ncols):
            jobs = [(wmat[oc * 128:(oc + 1) * 128, :], KC, oc) for oc in range(KC)]

            def comp(j, unit, oc):
                pi = next_psD()
                dense_mm(unit, KC, lambda k: z[:, k, :], pi, ncols, zkeys)
                vop(lambda: nc.vector.tensor_tensor(out=h[:, oc, 0:ncols], in0=h[:, oc, 0:ncols], in1=psD[pi][:, 0:ncols], op=ALU.add),
                    r=(f"psD{pi}", f"h{oc}"), w=(f"h{oc}",))
            run_jobs(jobs, comp)

        def transpose_cols_out(src_fn, nrows_tok, dst_dram_rows, tagkeys):
            for g4 in range(8):
                pm = g4 % 2
                fns = []
                for q4 in range(4):
                    kc = g4 * 4 + q4
                    fns.append(lambda kc=kc, q4=q4: nc.tensor.transpose(psM[pm][0:nrows_tok, q4 * 128:(q4 + 1) * 128], src_fn(kc), ident[:]))
                pgrp(fns, r=tuple(tagkeys) + ("const",), w=(f"psM{pm}",))
                aop(lambda g4=g4, pm=pm: nc.scalar.activation(out=ostage[0:nrows_tok, g4 % 4, :], in_=psM[pm][0:nrows_tok, :], func=AF.Copy),
                    r=(f"psM{pm}",), w=(f"tf{g4 % 4}",) if False else TFK)
                if g4 % 4 == 3:
                    c0 = (g4 // 4) * 2048
                    sy.dma(S, dst_dram_rows[:, c0:c0 + 2048], ost_flat[0:nrows_tok, :], reads=TFK, writes=("dram_out",))

        def load_cols(dst, dram2d, rows):
            sy.dma(S, pstage[0:rows, :], dram2d, writes=("pstage",))
            pgrp([lambda: nc.tensor.transpose(psM[0][:, 0:rows], pstage[0:rows, :], ident[0:rows, 0:rows])],
                 r=("pstage", "const"), w=("psM0",))
            vop(lambda: nc.vector.tensor_copy(out=dst[:, 0:rows], in_=psM[0][:, 0:rows]), r=("psM0",), w=("const",))

        def conv_mixer(p, ncols):
            barrier()
            cv = Carver([mx[:]])
            csb = cv.f32(2 * KC * NS, (2, KC, NS))
            cspre = cv.f32(KC * NS, (KC, NS))
            if p == 0:
                for r in range(2):
                    for half in range(2):
                        sy.dma(S, ost_flat[0:NS, :], i_conv[:, r, half * 2048:(half + 1) * 2048], writes=TFK)
                        fns = []
                        for q in range(16):
                            fns.append(lambda q=q: nc.tensor.transpose(psM[1][:, q * NS:(q + 1) * NS], ost_flat[0:NS, q * 128:(q + 1) * 128], ident[0:NS, 0:NS]))
                        pgrp(fns, r=TFK + ("const",), w=("psM1",))
                        vop(lambda r=r, half=half: nc.vector.tensor_copy(out=csb[:, r, half * 16:(half + 1) * 16, :],
                                                                         in_=psM[1][:, 0:16 * NS].rearrange("p (a b) -> p a b", b=NS)),
                            r=("psM1",), w=("csb",))
            W = conv_w_in
            jobs = []
            for oc in range(KC):
                jobs.append((W[D + oc * 128:D + (oc + 1) * 128, :], KC, ("gc", oc)))
                jobs.append((W[2 * D + oc * 128:2 * D + (oc + 1) * 128, :], KC, ("v", oc)))
                jobs.append((W[oc * 128:(oc + 1) * 128, :], KC, ("gb", oc)))
            st = {}
            pre = tf[2]
            t = tf[3]

            def comp(j, unit, tag):
                kind, oc = tag
                pi = next_psD()
                dense_mm(unit, KC, lambda k: u[:, k, :], pi, ncols, ukeys)
                st[kind] = pi
                if kind != "gb":
                    return
                pgc, pv, pgb = st["gc"], st["v"], st["gb"]
                w0 = pc_convw[:, oc:oc + 1]; w1 = pc_convw[:, 32 + oc:33 + oc]; w2 = pc_convw[:, 64 + oc:65 + oc]
                aop(lambda: nc.scalar.activation(out=tf[1][:, 0:ncols], in_=psD[pv][:, 0:ncols], func=AF.Copy), r=(f"psD{pv}",), w=("tf1",))
                vop(lambda: nc.vector.tensor_copy(out=pre[:, 0:2], in_=ccarry[:, :, oc]), r=("ccarry",), w=("tf2",))
                vop(lambda: nc.vector.tensor_tensor(out=pre[:, 2:2 + ncols], in0=tf[1][:, 0:ncols], in1=psD[pgc][:, 0:ncols], op=ALU.mult),
                    r=("tf1", f"psD{pgc}"), w=("tf2",))
                vop(lambda: nc.vector.tensor_scalar(out=t[:, 0:512], in0=pre[:, 0:512], scalar1=w0, scalar2=None, op0=ALU.mult),
                    r=("tf2", "const"), w=("tf3",))
                vop(lambda: nc.vector.scalar_tensor_tensor(out=t[:, 0:512], in0=pre[:, 1:513], scalar=w1, in1=t[:, 0:512], op0=ALU.mult, op1=ALU.add),
                    r=("tf2", "tf3", "const"), w=("tf3",))
                vop(lambda: nc.vector.scalar_tensor_tensor(out=t[:, 0:512], in0=pre[:, 2:514], scalar=w2, in1=t[:, 0:512], op0=ALU.mult, op1=ALU.add),
                    r=("tf2", "tf3", "const"), w=("tf3",))
                vop(lambda: nc.vector.tensor_copy(out=ccarry[:, :, oc], in_=pre[:, 512:514]), r=("tf2",), w=("ccarry",))
                if ncols > 512:
                    vop(lambda: nc.vector.tensor_scalar(out=t[:, 512:528], in0=csb[:, 0, oc, :], scalar1=w0, scalar2=None, op0=ALU.mult),
                        r=("csb", "const"), w=("tf3",))
                    vop(lambda: nc.vector.scalar_tensor_tensor(out=t[:, 512:528], in0=csb[:, 1, oc, :], scalar=w1, in1=t[:, 512:528], op0=ALU.mult, op1=ALU.add),
                        r=("csb", "tf3", "const"), w=("tf3",))
                    vop(lambda: nc.vector.scalar_tensor_tensor(out=t[:, 512:528], in0=pre[:, 514:530], scalar=w2, in1=t[:, 512:528], op0=ALU.mult, op1=ALU.add),
                        r=("tf2", "tf3", "const"), w=("tf3",))
                    vop(lambda: nc.vector.tensor_copy(out=cspre[:, oc, :], in_=pre[:, 514:530]), r=("tf2",), w=("cspre",))
                vop(lambda: nc.vector.tensor_tensor(out=z[:, oc, 0:ncols], in0=t[:, 0:ncols], in1=psD[pgb][:, 0:ncols], op=ALU.mult),
                    r=("tf3", f"psD{pgb}"), w=(f"z{oc}",))
            run_jobs(jobs, comp)
            if p == 0:
                transpose_cols_out(lambda kc: csb[:, 1, kc, :], NS, o_sconv[:, 0, :], ("csb",))
                transpose_cols_out(lambda kc: cspre[:, kc, :], NS, o_sconv[:, 1, :], ("cspre",))
            out_proj_residual(conv_w_out, ncols)
            barrier()

        def cmul(o_re, o_im, a_re_, a_im_, b_re_, b_im_, t1, t2, k1, k2, keys_r, keys_w):
            keys_r = tuple(keys_r); keys_w = tuple(keys_w)
            vop(lambda: nc.vector.tensor_tensor(out=t1, in0=a_re_, in1=b_re_, op=ALU.mult), r=keys_r, w=(k1,))
            vop(lambda: nc.vector.tensor_tensor(out=t2, in0=a_im_, in1=b_im_, op=ALU.mult), r=keys_r, w=(k2,))
            vop(lambda: nc.vector.tensor_tensor(out=o_re, in0=t1, in1=t2, op=ALU.subtract), r=(k1, k2) + keys_r, w=keys_w)
            vop(lambda: nc.vector.tensor_tensor(out=t1, in0=a_re_, in1=b_im_, op=ALU.mult), r=keys_r + keys_w, w=(k1,))
            vop(lambda: nc.vector.tensor_tensor(out=t2, in0=a_im_, in1=b_re_, op=ALU.mult), r=keys_r + keys_w, w=(k2,))
            vop(lambda: nc.vector.tensor_tensor(out=o_im, in0=t1, in1=t2, op=ALU.add), r=(k1, k2), w=keys_w)

        def sincos(src_turns, o_sin, o_cos, ki_ap, fr, n, key):
            K_ = (key,)
            vop(lambda: nc.vector.tensor_copy(out=ki_ap, in_=src_turns), r=K_, w=K_)
            vop(lambda: nc.vector.tensor_tensor(out=fr, in0=src_turns, in1=ki_ap, op=ALU.subtract), r=K_, w=K_)
            aop(lambda: nc.scalar.activation(out=o_sin, in_=fr, func=AF.Sin, scale=2 * PI), r=K_, w=K_)
            aop(lambda: nc.scalar.activation(out=o_cos, in_=fr, func=AF.Sin, scale=PI), r=K_, w=K_)
            vop(lambda: nc.vector.tensor_tensor(out=o_cos, in0=o_cos, in1=o_cos, op=ALU.mult), r=K_, w=K_)
            vop(lambda: nc.vector.tensor_scalar(out=o_cos, in0=o_cos, scalar1=-2.0, scalar2=1.0, op0=ALU.mult, op1=ALU.add), r=K_, w=K_)

        def s5_setup(j):
            barrier()
            hflat = h[:].rearrange("p a b -> p (a b)")
            uflat = u[:].rearrange("p a b -> p (a b)").bitcast(F32)
            zflat = z[:].rearrange("p a b -> p (a b)").bitcast(F32)
            cv = Carver([hflat, uflat, zflat])
            SU = ("su",)

            def sq_(n=128):
                return cv.f32(n)
            nat = sq_(); AR = sq_(); AI = sq_(); LDT = sq_(); ldt2 = sq_(2)
            DT = sq_(); ALPHA = sq_(); THETA = sq_(); THT = sq_(); MAG = sq_(); SIN = sq_(); COS = sq_(); FR = sq_(); KI = sq_().bitcast(I32)
            LRE = sq_(); LIM = sq_(); DEN = sq_(); RDEN = sq_(); LM1 = sq_(); CORE = sq_(); COIM = sq_(); TA = sq_(); TB = sq_()
            tb5 = cv.f32(5 * 128, (5, 128))
            rm0 = sq_(1); rm1 = sq_(1)

            def su_v(fn):
                return vop(fn, r=SU, w=SU)

            def su_a(fn):
                return aop(fn, r=SU, w=SU)

            def load_T(dst, dram2d_fn):
                dram2d_fn()
                pgrp([lambda: nc.tensor.transpose(psM[0][:, 0:128], nat, ident[:])], r=SU + ("const",), w=("psM0",))
                vop(lambda: nc.vector.tensor_copy(out=dst, in_=psM[0][:, 0:128]), r=("psM0",), w=SU)

            load_T(AR, lambda: sy.dma(S, nat, a_re[j].rearrange("(i jj) p -> i (jj p)", jj=2), reads=SU, writes=SU))
            load_T(AI, lambda: sy.dma(S, nat, a_im[j].rearrange("(i jj) p -> i (jj p)", jj=2), reads=SU, writes=SU))
            sy.dma(S, ldt2, log_dt[j].rearrange("(i jj) -> i jj", jj=2), reads=SU, writes=SU)

            def fill_ldt():
                for jj in range(2):
                    su_v(lambda jj=jj: nc.vector.tensor_scalar(out=nat[:, jj * 64:(jj + 1) * 64], in0=onesf[:, 0:64], scalar1=ldt2[:, jj:jj + 1],
                                                              scalar2=None, op0=ALU.mult))
            load_T(LDT, fill_ldt)
            su_a(lambda: nc.scalar.activation(out=DT, in_=LDT, func=AF.Exp))
            su_v(lambda: nc.vector.tensor_tensor(out=ALPHA, in0=AR, in1=DT, op=ALU.mult))
            su_v(lambda: nc.vector.tensor_tensor(out=THETA, in0=AI, in1=DT, op=ALU.mult))
            su_v(lambda: nc.vector.tensor_scalar(out=THT, in0=THETA, scalar1=1.0 / (2 * PI), scalar2=None, op0=ALU.mult))
            su_a(lambda: nc.scalar.activation(out=MAG, in_=ALPHA, func=AF.Exp))
            sincos(THT, SIN, COS, KI, FR, 128, "su")
            su_v(lambda: nc.vector.tensor_tensor(out=LRE, in0=MAG, in1=COS, op=ALU.mult))
            su_v(lambda: nc.vector.tensor_tensor(out=LIM, in0=MAG, in1=SIN, op=ALU.mult))
            su_v(lambda: nc.vector.tensor_copy(out=tb5[:, 0, :], in_=LRE))
            su_v(lambda: nc.vector.tensor_copy(out=tb5[:, 1, :], in_=LIM))
            su_v(lambda: nc.vector.tensor_scalar(out=tb5[:, 2, :], in0=LIM, scalar1=-1.0, scalar2=None, op0=ALU.mult))
            su_a(lambda: nc.scalar.activation(out=TA, in_=ALPHA, func=AF.Exp, scale=float(TP - 1)))
            su_v(lambda: nc.vector.tensor_scalar(out=TB, in0=THT, scalar1=float(TP - 1), scalar2=None, op0=ALU.mult))
            sincos(TB, SIN, COS, KI, FR, 128, "su")
            su_v(lambda: nc.vector.tensor_tensor(out=tb5[:, 3, :], in0=TA, in1=COS, op=ALU.mult))
            su_v(lambda: nc.vector.tensor_tensor(out=tb5[:, 4, :], in0=TA, in1=SIN, op=ALU.mult))
            sy.dma(S, s5t[j].rearrange("p (a b) -> p a b", b=128), tb5, reads=SU, writes=("s5t",))
            su_v(lambda: nc.vector.tensor_tensor(out=DEN, in0=AR, in1=AR, op=ALU.mult))
            su_v(lambda: nc.vector.tensor_tensor(out=TA, in0=AI, in1=AI, op=ALU.mult))
            su_v(lambda: nc.vector.tensor_tensor(out=DEN, in0=DEN, in1=TA, op=ALU.add))
            su_v(lambda: nc.vector.reciprocal(out=RDEN, in_=DEN))
            su_v(lambda: nc.vector.tensor_scalar(out=LM1, in0=LRE, scalar1=-1.0, scalar2=None, op0=ALU.add))
            su_v(lambda: nc.vector.tensor_tensor(out=TA, in0=LM1, in1=AR, op=ALU.mult))
            su_v(lambda: nc.vector.tensor_tensor(out=TB, in0=LIM, in1=AI, op=ALU.mult))
            su_v(lambda: nc.vector.tensor_tensor(out=TA, in0=TA, in1=TB, op=ALU.add))
            su_v(lambda: nc.vector.tensor_tensor(out=CORE, in0=TA, in1=RDEN, op=ALU.mult))
            su_v(lambda: nc.vector.tensor_tensor(out=TA, in0=LIM, in1=AR, op=ALU.mult))
            su_v(lambda: nc.vector.tensor_tensor(out=TB, in0=LM1, in1=AI, op=ALU.mult))
            su_v(lambda: nc.vector.tensor_tensor(out=TA, in0=TA, in1=TB, op=ALU.subtract))
            su_v(lambda: nc.vector.tensor_tensor(out=COIM, in0=TA, in1=RDEN, op=ALU.mult))
            mk = cv.mark()
            Braw_re = cv.f32(2048, (128, 16)); Braw_im = cv.f32(2048, (128, 16))
            bb_re = cv.f32(2048, (128, 16)); bb_im = cv.f32(2048, (128, 16)); btmp = cv.f32(2048, (128, 16))
            bbpad_flat = cv.f32(4096)
            bbpad = bbpad_flat.rearrange("p (k a j c) -> p k a j c", k=32, a=4, j=2, c=16)
            for jj in range(2):
                for i8 in range(8):
                    isl = slice(i8 * 16, (i8 + 1) * 16)
                    sy.dma(S, Braw_re[jj * 64:(jj + 1) * 64, isl, :], b_re[j].rearrange("(i jj) p c -> jj p i c", jj=2)[jj][:, isl, :], reads=SU, writes=SU)
                    sy.dma(S, Braw_im[jj * 64:(jj + 1) * 64, isl, :], b_im[j].rearrange("(i jj) p c -> jj p i c", jj=2)[jj][:, isl, :], reads=SU, writes=SU)
            core_b = CORE.unsqueeze(2).to_broadcast([128, 128, 16])
            coim_b = COIM.unsqueeze(2).to_broadcast([128, 128, 16])
            su_v(lambda: nc.vector.tensor_tensor(out=bb_re, in0=Braw_re, in1=core_b, op=ALU.mult))
            su_v(lambda: nc.vector.tensor_tensor(out=btmp, in0=Braw_im, in1=coim_b, op=ALU.mult))
            su_v(lambda: nc.vector.tensor_tensor(out=bb_re, in0=bb_re, in1=btmp, op=ALU.subtract))
            su_v(lambda: nc.vector.tensor_tensor(out=bb_im, in0=Braw_im, in1=core_b, op=ALU.mult))
            su_v(lambda: nc.vector.tensor_tensor(out=btmp, in0=Braw_re, in1=coim_b, op=ALU.mult))
            su_v(lambda: nc.vector.tensor_tensor(out=bb_im, in0=bb_im, in1=btmp, op=ALU.add))

            def emit_unit(src_fn, unit_idx, scale):
                for g in range(8):
                    pd = g % 2
                    fns = []
                    for q in range(4):
                        fns.append(lambda g=g, q=q: nc.tensor.transpose(psD[pd][:, q * 128:(q + 1) * 128], src_fn(g * 4 + q), ident[:]))
                    pgrp(fns, r=SU + ("const",), w=(f"psD{pd}",))
                    aop(lambda g=g, pd=pd: nc.scalar.mul(wb[0][:, g * 4:(g + 1) * 4, :], psD[pd][:, 0:512].rearrange("p (a b) -> p a b", b=128), scale),
                        r=(f"psD{pd}",), w=("wb0",))
                sy.dma(S, s5w[j, unit_idx], wb[0][:].rearrange("p a b -> p (a b)"), reads=("wb0",), writes=("s5w",))

            for bb, ui in ((bb_re, 0), (bb_im, 1)):
                su_v(lambda: nc.vector.memset(bbpad_flat, 0.0))
                for jj in range(2):
                    su_v(lambda jj=jj, bb=bb: nc.vector.tensor_copy(out=bbpad[jj * 64:(jj + 1) * 64, :, :, jj, :],
                                                                    in_=bb[jj * 64:(jj + 1) * 64].rearrange("p (k a) c -> p k a c", a=4)))
                emit_unit(lambda kc: bbpad[:, kc].rearrange("p a b c -> p (a b c)"), ui, 1.0)
            barrier()
            cv.reset(mk)
            rrow = sq_(128)
            gop(lambda: nc.gpsimd.memset(rrow[0:1, :], 0.0), r=SU, w=SU)
            for b4 in range(4):
                gop(lambda b4=b4: nc.gpsimd.memset(rrow[0:1, b4 * 32:b4 * 32 + 16], 1.0), r=SU, w=SU)
            pgrp([lambda: nc.tensor.transpose(psM[0][:, 0:1], rrow[0:1, :], ident[0:1, 0:1])], r=SU + ("const",), w=("psM0",))
            vop(lambda: nc.vector.tensor_copy(out=rm0, in_=psM[0][:, 0:1]), r=("psM0",), w=SU)
            su_v(lambda: nc.vector.tensor_scalar(out=rm1, in0=rm0, scalar1=-1.0, scalar2=1.0, op0=ALU.mult, op1=ALU.add))
            Cnat = cv.f32(2048, (32, 64)); Cbd = cv.f32(4096, (32, 128))
            for cdram, ui, scale in ((c_re, 2, 1.0), (c_im, 3, -1.0)):
                for g8 in range(8):
                    sy.dma(S, Cnat[g8 * 16:(g8 + 1) * 16], cdram[j].rearrange("(k g8) c p -> g8 c k p", g8=8)[g8], reads=SU, writes=SU)
                su_v(lambda: nc.vector.tensor_scalar(out=Cbd[:, :, 0:64], in0=Cnat, scalar1=rm0[:, 0:1], scalar2=None, op0=ALU.mult))
                su_v(lambda: nc.vector.tensor_scalar(out=Cbd[:, :, 64:128], in0=Cnat, scalar1=rm1[:, 0:1], scalar2=None, op0=ALU.mult))
                emit_unit(lambda kc: Cbd[:, kc, :], ui, scale)
            barrier()
            cv.reset(mk)
            iota_i = cv.f32(512).bitcast(I32); iota = cv.f32(512)
            gop(lambda: nc.gpsimd.iota(iota_i, pattern=[[1, 512]], base=0, channel_multiplier=0), r=SU, w=SU)
            su_v(lambda: nc.vector.tensor_copy(out=iota, in_=iota_i))
            tabs = [cv.f32(2048, (4, 512)) for _ in range(2)]
            Ttn = cv.f32(512); Tfr = cv.f32(512); Tki = cv.f32(512).bitcast(I32); Tsin = cv.f32(512); Tcos = cv.f32(512); Tmag = cv.f32(512); Timag = cv.f32(512)
            NAL = sq_()
            su_v(lambda: nc.vector.tensor_scalar(out=NAL, in0=ALPHA, scalar1=-1.0, scalar2=None, op0=ALU.mult))
            for i in range(128):
                tab = tabs[i % 2]
                k = f"tab{i % 2}"
                vop(lambda i=i: nc.vector.tensor_scalar(out=Ttn, in0=iota, scalar1=THT[:, i:i + 1], scalar2=None, op0=ALU.mult), r=SU, w=("tt",))
                vop(lambda: nc.vector.tensor_copy(out=Tki, in_=Ttn), r=("tt",), w=("tki",))
                vop(lambda: nc.vector.tensor_tensor(out=Tfr, in0=Ttn, in1=Tki, op=ALU.subtract), r=("tt", "tki"), w=("tfr",))
                aop(lambda: nc.scalar.activation(out=Tsin, in_=Tfr, func=AF.Sin, scale=2 * PI), r=("tfr",), w=("tsin",))
                aop(lambda: nc.scalar.activation(out=Tcos, in_=Tfr, func=AF.Sin, scale=PI), r=("tfr",), w=("tcos",))
                aop(lambda i=i: nc.scalar.activation(out=Tmag, in_=iota, func=AF.Exp, scale=ALPHA[:, i:i + 1]), r=SU, w=("tmag",))
                aop(lambda i=i: nc.scalar.activation(out=Timag, in_=iota, func=AF.Exp, scale=NAL[:, i:i + 1]), r=SU, w=("timag",))
                gop(lambda: nc.gpsimd.tensor_tensor(out=Tcos, in0=Tcos, in1=Tcos, op=ALU.mult), r=("tcos",), w=("tcos",))
                gop(lambda: nc.gpsimd.tensor_scalar(out=Tcos, in0=Tcos, scalar1=-2.0, scalar2=1.0, op0=ALU.mult, op1=ALU.add), r=("tcos",), w=("tcos",))
                vop(lambda tab=tab: nc.vector.tensor_tensor(out=tab[:, 0, :], in0=Tcos, in1=Timag, op=ALU.mult), r=("tcos", "timag"), w=(k,))
                vop(lambda tab=tab: nc.vector.scalar_tensor_tensor(out=tab[:, 1, :], in0=Tsin, scalar=-1.0, in1=Timag, op0=ALU.mult, op1=ALU.mult),
                    r=("tsin", "timag", k), w=(k,))
                gop(lambda tab=tab: nc.gpsimd.tensor_tensor(out=tab[:, 2, :], in0=Tcos, in1=Tmag, op=ALU.mult), r=("tcos", "tmag", k), w=(k,))
                gop(lambda tab=tab: nc.gpsimd.tensor_tensor(out=tab[:, 3, :], in0=Tsin, in1=Tmag, op=ALU.mult), r=("tsin", "tmag", k), w=(k,))
                sy.dma(S, s5tab[j, i].rearrange("p (a b) -> p a b", b=512), tab, reads=(k,), writes=("s5tab",))
            barrier()

        def s5_mixer(j, layer, p, ncols):
            barrier()
            cv = Carver([mx[:]])
            Gt = cv.f32(1024, (2, 512)); Ft = cv.f32(1024, (2, 512))
            X1 = cv.bf16(528); X2 = cv.bf16(528)
            tb5 = cv.f32(5 * 128, (5, 128))
            cre = cv.f32(128); cim = cv.f32(128); zlre = cv.f32(128); zlim = cv.f32(128)
            stg = [cv.f32(512, (4, 128)) for _ in range(2)]
            hsb = [cv.f32(64, (4, NS)) for _ in range(2)]
            xsn = [cv.f32(64, (4, NS)) for _ in range(2)]
            uz = [cv.bf16(528) for _ in range(2)]
            for q in range(2):
                vop(lambda q=q: nc.vector.memset(uz[q][64:96, :], 0.0), w=(f"uz{q}",))
            W1, W2, W3, W4 = tf
            LRE, LIM, NLIM, FLRE, FLIM = (tb5[:, q, :] for q in range(5))
            hre, him = hst[j][0][:], hst[j][1][:]
            i_st = (i_s5re, i_s5im); o_st = (o_sre, o_sim)
            for q in range(4):
                sy.dma(S, wb[q][:].rearrange("p a b -> p (a b)"), s5w[j, q], reads=("s5w",), writes=(f"wb{q}",))
            sy.dma(S, tb5, s5t[j].rearrange("p (a b) -> p a b", b=128), reads=("s5t",), writes=("tb5",))
            if p == 0:
                vop(lambda: nc.vector.memset(cre, 0.0), w=("cc",))
                vop(lambda: nc.vector.memset(cim, 0.0), w=("cc",))
            else:
                cmul(cre, cim, LRE, LIM, hre, him, W3[:, 0:128], W4[:, 0:128], "tf2", "tf3", ("tb5", f"hst{j}"), ("cc",))
            dcol = lambda kc: pc_s5d[:, j * 32 + kc:j * 32 + kc + 1]
            wcol = lambda kc: pc_nmix[:, layer * 32 + kc:layer * 32 + kc + 1]
            for kc in range(KC):
                if p == 0:
                    for r in range(2):
                        sy.dma(S, stg[r][0:NS], i_st[r][j].rearrange("n (i jj) p -> n i (jj p)", jj=2)[:, kc * 4:(kc + 1) * 4, :], writes=(f"stg{r}",))
                        pgrp([lambda r=r, q=q: nc.tensor.transpose(psM[r][:, q * NS:(q + 1) * NS], stg[r][0:NS, q, :], ident[0:NS, 0:NS]) for q in range(4)],
                             r=(f"stg{r}", "const"), w=(f"psM{r}",))
                        vop(lambda r=r: nc.vector.tensor_copy(out=hsb[r], in_=psM[r][:, 0:4 * NS].rearrange("p (a b) -> p a b", b=NS)),
                            r=(f"psM{r}",), w=(f"hsb{r}",))
                uzt = uz[kc % 2]
                uzk = f"uz{kc % 2}"
                sy.dma(S, uzt[96:128, 0:ncols], u[96:128, kc, 0:ncols], reads=(f"u{kc}",), writes=(uzk,))
                for i4 in (0, 1, 3, 2):
                    i = kc * 4 + i4
                    a = i % 2
                    if i4 < 3:
                        rows = slice(i4 * 32, (i4 + 1) * 32)
                        usrc = u[rows, kc, :]
                        ukey = f"u{kc}"
                    else:
                        rows = slice(64, 128)
                        usrc = uzt[rows, :]
                        ukey = uzk
                    sy.dma(S, Gt, s5tab[j, i][:, 0:1024].rearrange("p (a b) -> p a b", b=512), reads=("s5tab",), writes=("Gt",))
                    sy.dma(S, Ft, s5tab[j, i][:, 1024:2048].rearrange("p (a b) -> p a b", b=512), reads=("s5tab",), writes=("Ft",))
                    pgrp([lambda: nc.tensor.matmul(psD[a][:, 0:512], wb[0][rows, kc, :], usrc[:, 0:512], start=True, stop=True),
                          lambda: nc.tensor.matmul(psD[a][:, 512:1024], wb[1][rows, kc, :], usrc[:, 0:512], start=True, stop=True)],
                         r=(ukey, "wb0", "wb1"), w=(f"psD{a}",))
                    bre = psD[a][:, 0:512]; bim = psD[a][:, 512:1024]
                    Gre, Gim, Fre, Fim = Gt[:, 0, :], Gt[:, 1, :], Ft[:, 0, :], Ft[:, 1, :]
                    pk = f"psD{a}"
                    vop(lambda: nc.vector.tensor_tensor(out=W1[:, 0:512], in0=bre, in1=Gre, op=ALU.mult), r=(pk, "Gt"), w=("tf0",))
                    vop(lambda: nc.vector.tensor_tensor(out=W2[:, 0:512], in0=bim, in1=Gim, op=ALU.mult), r=(pk, "Gt"), w=("tf1",))
                    vop(lambda: nc.vector.tensor_tensor(out=W1[:, 0:512], in0=W1[:, 0:512], in1=W2[:, 0:512], op=ALU.subtract), r=("tf0", "tf1"), w=("tf0",))
                    vop(lambda: nc.vector.tensor_tensor(out=W2[:, 0:512], in0=bre, in1=Gim, op=ALU.mult), r=(pk, "Gt", "tf1"), w=("tf1",))
                    vop(lambda: nc.vector.tensor_tensor(out=W3[:, 0:512], in0=bim, in1=Gre, op=ALU.mult), r=(pk, "Gt"), w=("tf2",))
                    vop(lambda: nc.vector.tensor_tensor(out=W2[:, 0:512], in0=W2[:, 0:512], in1=W3[:, 0:512], op=ALU.add), r=("tf1", "tf2"), w=("tf1",))
                    vop(lambda i=i: nc.vector.tensor_tensor_scan(out=W3[:, 0:512], data0=onesf[:, 0:1].to_broadcast([128, 512]), data1=W1[:, 0:512],
                                                                 initial=cre[:, i:i + 1], op0=ALU.mult, op1=ALU.add), r=("tf0", "cc", "const"), w=("tf2",))
                    vop(lambda i=i: nc.vector.tensor_tensor_scan(out=W4[:, 0:512], data0=onesf[:, 0:1].to_broadcast([128, 512]), data1=W2[:, 0:512],
                                                                 initial=cim[:, i:i + 1], op0=ALU.mult, op1=ALU.add), r=("tf1", "cc", "const"), w=("tf3",))
                    aop(lambda i=i: nc.scalar.activation(out=zlre[:, i:i + 1], in_=W3[:, 511:512], func=AF.Copy), r=("tf2",), w=("zl",))
                    aop(lambda i=i: nc.scalar.activation(out=zlim[:, i:i + 1], in_=W4[:, 511:512], func=AF.Copy), r=("tf3",), w=("zl",))
                    vop(lambda: nc.vector.tensor_tensor(out=W1[:, 0:512], in0=Fre, in1=W3[:, 0:512], op=ALU.mult), r=("Ft", "tf2"), w=("tf0",))
                    vop(lambda: nc.vector.tensor_tensor(out=W2[:, 0:512], in0=Fim, in1=W4[:, 0:512], op=ALU.mult), r=("Ft", "tf3"), w=("tf1",))
                    vop(lambda: nc.vector.tensor_tensor(out=X1[:, 0:512], in0=W1[:, 0:512], in1=W2[:, 0:512], op=ALU.subtract), r=("tf0", "tf1"), w=("X1",))
                    vop(lambda: nc.vector.tensor_tensor(out=W1[:, 0:512], in0=Fre, in1=W4[:, 0:512], op=ALU.mult), r=("Ft", "tf3", "X1"), w=("tf0",))
                    vop(lambda: nc.vector.tensor_tensor(out=W2[:, 0:512], in0=Fim, in1=W3[:, 0:512], op=ALU.mult), r=("Ft", "tf2", "X1"), w=("tf1",))
                    vop(lambda: nc.vector.tensor_tensor(out=X2[:, 0:512], in0=W1[:, 0:512], in1=W2[:, 0:512], op=ALU.add), r=("tf0", "tf1"), w=("X2",))
                    if p == 0:
                        pgrp([lambda: nc.tensor.matmul(psM[a][:, 0:NS], wb[0][rows, kc, :], usrc[:, 512:528], start=True, stop=True),
                              lambda: nc.tensor.matmul(psM[a][:, NS:2 * NS], wb[1][rows, kc, :], usrc[:, 512:528], start=True, stop=True)],
                             r=(ukey, "wb0", "wb1"), w=(f"psM{a}",))
                        pm = f"psM{a}"
                        vop(lambda i=i, i4=i4: nc.vector.scalar_tensor_tensor(out=xsn[0][:, i4, :], in0=hsb[0][:, i4, :], scalar=LRE[:, i:i + 1], in1=psM[a][:, 0:NS],
                                                                             op0=ALU.mult, op1=ALU.add), r=("hsb0", "tb5", pm), w=("xsn0",))
                        vop(lambda i=i, i4=i4: nc.vector.scalar_tensor_tensor(out=xsn[0][:, i4, :], in0=hsb[1][:, i4, :], scalar=NLIM[:, i:i + 1], in1=xsn[0][:, i4, :],
                                                                             op0=ALU.mult, op1=ALU.add), r=("hsb1", "tb5", "xsn0"), w=("xsn0",))
                        vop(lambda i=i, i4=i4: nc.vector.scalar_tensor_tensor(out=xsn[1][:, i4, :], in0=hsb[1][:, i4, :], scalar=LRE[:, i:i + 1], in1=psM[a][:, NS:2 * NS],
                                                                             op0=ALU.mult, op1=ALU.add), r=("hsb1", "tb5", pm), w=("xsn1",))
                        vop(lambda i=i, i4=i4: nc.vector.scalar_tensor_tensor(out=xsn[1][:, i4, :], in0=hsb[0][:, i4, :], scalar=LIM[:, i:i + 1], in1=xsn[1][:, i4, :],
                                                                             op0=ALU.mult, op1=ALU.add), r=("hsb0", "tb5", "xsn1"), w=("xsn1",))
                        vop(lambda i4=i4: nc.vector.tensor_copy(out=X1[:, 512:528], in_=xsn[0][:, i4, :]), r=("xsn0",), w=("X1",))
                        vop(lambda i4=i4: nc.vector.tensor_copy(out=X2[:, 512:528], in_=xsn[1][:, i4, :]), r=("xsn1",), w=("X2",))
                    cols = slice(i4 * 32, (i4 + 1) * 32) if i4 < 3 else slice(64, 128)
                    fns = [lambda: nc.tensor.matmul(psD[2][cols, 0:512], wb[2][:, kc, cols], X1[:, 0:512], start=True, stop=False),
                           lambda: nc.tensor.matmul(psD[2][cols, 0:512], wb[3][:, kc, cols], X2[:, 0:512], start=False, stop=True)]
                    if p == 0:
                        fns += [lambda: nc.tensor.matmul(psD[2][cols, 512:528], wb[2][:, kc, cols], X1[:, 512:528], start=True, stop=False),
                                lambda: nc.tensor.matmul(psD[2][cols, 512:528], wb[3][:, kc, cols], X2[:, 512:528], start=False, stop=True)]
                    pgrp(fns, r=("X1", "X2", "wb2", "wb3"), w=("psD2",), acc=(i4 > 0))
                vop(lambda kc=kc: nc.vector.scalar_tensor_tensor(out=W1[:, 0:ncols], in0=h[:, kc, 0:ncols], scalar=wcol(kc), in1=rstd[:, 0:ncols],
                                                                 op0=ALU.mult, op1=ALU.mult), r=(f"h{kc}", "rstd", "const"), w=("tf0",))
                vop(lambda kc=kc: nc.vector.scalar_tensor_tensor(out=W1[:, 0:ncols], in0=W1[:, 0:ncols], scalar=dcol(kc), in1=psD[2][:, 0:ncols],
                                                                 op0=ALU.mult, op1=ALU.add), r=("tf0", "psD2", "const"), w=("tf0",))
                gop(lambda: nc.gpsimd.tensor_tensor(out=W2[:, 0:ncols], in0=W1[:, 0:ncols], in1=W1[:, 0:ncols], op=ALU.mult), r=("tf0",), w=("tf1",))
                gop(lambda: nc.gpsimd.tensor_scalar(out=W2[:, 0:ncols], in0=W2[:, 0:ncols], scalar1=0.044715, scalar2=1.0, op0=ALU.mult, op1=ALU.add), r=("tf1",), w=("tf1",))
                gop(lambda: nc.gpsimd.tensor_tensor(out=W2[:, 0:ncols], in0=W2[:, 0:ncols], in1=W1[:, 0:ncols], op=ALU.mult), r=("tf0", "tf1"), w=("tf1",))
                aop(lambda: nc.scalar.activation(out=W2[:, 0:ncols], in_=W2[:, 0:ncols], func=AF.Sigmoid, scale=1.5957691216057308), r=("tf1",), w=("tf1",))
                vop(lambda kc=kc: nc.vector.tensor_tensor(out=z[:, kc, 0:ncols], in0=W1[:, 0:ncols], in1=W2[:, 0:ncols], op=ALU.mult), r=("tf0", "tf1"), w=(f"z{kc}",))
                if p == 0:
                    for r in range(2):
                        pgrp([lambda r=r, q=q: nc.tensor.transpose(psM[r][0:NS, q * 128:(q + 1) * 128], xsn[r][:, q, :], ident[:]) for q in range(4)],
                             r=(f"xsn{r}", "const"), w=(f"psM{r}",))
                        vop(lambda r=r: nc.vector.tensor_copy(out=stg[r][0:NS].rearrange("p a b -> p (a b)"), in_=psM[r][0:NS, :]), r=(f"psM{r}",), w=(f"stg{r}",))
                        sy.dma(S, o_st[r][j].rearrange("n (i jj) p -> n i (jj p)", jj=2)[:, kc * 4:(kc + 1) * 4, :], stg[r][0:NS], reads=(f"stg{r}",), writes=("dram_out",))
            cmul(hre, him, FLRE, FLIM, zlre, zlim, W3[:, 0:128], W4[:, 0:128], "tf2", "tf3", ("tb5", "zl"), (f"hst{j}",))
            jobs = []
            for oc in range(KC):
                jobs.append((w_glu[j, oc * 128:(oc + 1) * 128, :], KC, ("a", oc)))
                jobs.append((w_glu[j, D + oc * 128:D + (oc + 1) * 128, :], KC, ("b", oc)))
            st = {}

            def comp(jx, unit, tag):
                kind, oc = tag
                pi = next_psD()
                dense_mm(unit, KC, lambda k: z[:, k, :], pi, ncols, zkeys)
                if kind == "a":
                    st["a"] = pi
                    return
                pa = st["a"]
                aop(lambda: nc.scalar.activation(out=tf[1][:, 0:ncols], in_=psD[pi][:, 0:ncols], func=AF.Sigmoid), r=(f"psD{pi}",), w=("tf1",))
                vop(lambda: nc.vector.tensor_tensor(out=tf[1][:, 0:ncols], in0=tf[1][:, 0:ncols], in1=psD[pa][:, 0:ncols], op=ALU.mult),
                    r=("tf1", f"psD{pa}"), w=("tf1",))
                gop(lambda: nc.gpsimd.tensor_tensor(out=h[:, oc, 0:ncols], in0=h[:, oc, 0:ncols], in1=tf[1][:, 0:ncols], op=ALU.add),
                    r=("tf1", f"h{oc}"), w=(f"h{oc}",))
            run_jobs(jobs, comp)
            barrier()

        def hgrn_mixer(p, ncols, last_pass):
            barrier()
            cv = Carver([mx[:]])
            T1, T2, T3, T4 = tf
            T5 = cv.f32(532); T6 = cv.f32(532); SG = cv.f32(532); O = cv.f32(532)
            QT = cv.bf16(528); KT = cv.bf16(528)
            VTOK = cv.bf16(512, (4, 128)); KHTOK = cv.bf16(512, (4, 128)); VTOKm = cv.bf16(512, (4, 128))
            m3 = cv.f32(1)
            vop(lambda: nc.vector.memset(m3, 1.0), w=("m3",))
            vop(lambda: nc.vector.memset(m3[64:96, :], 0.0), r=("m3",), w=("m3",))
            ATT = cv.bf16(128); Sst = cv.f32(128); SBc = [cv.bf16(128) for _ in range(4)]; EBL = cv.f32(16)
            SHG = cv.f32(1024, (8, 128))
            QF = cv.f32(NS); QK = cv.f32(NS); FS = cv.f32(NS); KS = cv.f32(NS); VTsb = cv.bf16(NS)
            W = hg_w_in
            jobs = []
            for hd in range(32):
                for qi, nm in enumerate(("q", "f", "iv", "g")):
                    jobs.append((W[qi * D + hd * 128:qi * D + (hd + 1) * 128, :], KC, (nm, hd)))
            DSB = [(0, 0), (0, 512), (1, 0), (1, 512)]

            def head_rest(hd):
                lbc = pc_lb[:, hd:hd + 1]; omlc = pc_oml[:, hd:hd + 1]; gnc = pc_gn[:, hd:hd + 1]
                if p == 0:
                    vop(lambda: nc.vector.memset(Sst, 0.0), w=("S",))
                else:
                    sy.dma(S, Sst, hgS[hd], reads=("hgS",), writes=("S",))
                vop(lambda: nc.vector.tensor_tensor_scan(out=T4[:, 0:512], data0=msk[:], data1=T3[:, 0:512], initial=0.0, op0=ALU.mult, op1=ALU.add),
                    r=("tf2", "const"), w=("tf3",))
                if ncols > 512:
                    vop(lambda: nc.vector.tensor_copy(out=T4[:, 512:528], in_=T3[:, 512:528]), r=("tf2",), w=("tf3",))
                aop(lambda: nc.scalar.activation(out=T5[:, 0:ncols], in_=T4[:, 0:ncols], func=AF.Exp), r=("tf3",), w=("T5",))
                vop(lambda: nc.vector.tensor_tensor(out=QT[:, 0:ncols], in0=T1[:, 0:ncols], in1=T5[:, 0:ncols], op=ALU.mult), r=("tf0", "T5"), w=("QT",))
                vop(lambda: nc.vector.tensor_copy(out=EBL, in_=T5[:, 0:512].rearrange("p (c l) -> p c l", l=32)[:, :, 31]), r=("T5",), w=("EBL",))
                if ncols > 512:
                    vop(lambda: nc.vector.tensor_tensor(out=QF, in0=T1[:, 512:528], in1=T5[:, 512:528], op=ALU.mult), r=("tf0", "T5"), w=("QF",))
                    vop(lambda: nc.vector.tensor_tensor(out=QK, in0=T1[:, 512:528], in1=T2[:, 512:528], op=ALU.mult), r=("tf0", "tf1"), w=("QK",))
                    vop(lambda: nc.vector.tensor_copy(out=KS, in_=T2[:, 512:528]), r=("tf1",), w=("KS",))
                    vop(lambda: nc.vector.tensor_scalar(out=FS, in0=T2[:, 512:528], scalar1=-1.0, scalar2=1.0, op0=ALU.mult, op1=ALU.add), r=("tf1",), w=("FS",))
                aop(lambda: nc.scalar.activation(out=T5[:, 0:512], in_=T4[:, 0:512], func=AF.Exp, scale=-1.0), r=("tf3", "QT", "EBL", "QF"), w=("T5",))
                vop(lambda: nc.vector.tensor_tensor(out=KT[:, 0:512], in0=T2[:, 0:512], in1=T5[:, 0:512], op=ALU.mult), r=("tf1", "T5"), w=("KT",))
                b3 = T4[:, 0:512].rearrange("p (c l) -> p c l", l=32)
                vop(lambda: nc.vector.tensor_tensor(out=T5[:, 0:512].rearrange("p (c l) -> p c l", l=32), in0=b3[:, :, 31:32].to_broadcast([128, 16, 32]), in1=b3,
                                                    op=ALU.subtract), r=("tf3", "KT"), w=("T5",))
                aop(lambda: nc.scalar.activation(out=T5[:, 0:512], in_=T5[:, 0:512], func=AF.Exp), r=("T5",), w=("T5",))
                vop(lambda: nc.vector.tensor_tensor(out=T6[:, 0:512], in0=T2[:, 0:512], in1=T5[:, 0:512], op=ALU.mult), r=("tf1", "T5"), w=("T6",))

            st = {}

            def comp(jx, unit, tag):
                nm, hd = tag
                pi = next_psD()
                dense_mm(unit, KC, lambda k: u[:, k, :], pi, ncols, ukeys)
                pk = f"psD{pi}"
                lbc = pc_lb[:, hd:hd + 1]; omlc = pc_oml[:, hd:hd + 1]; gnc = pc_gn[:, hd:hd + 1]
                if nm == "q":
                    aop(lambda: nc.scalar.activation(out=T1[:, 0:ncols], in_=psD[pi][:, 0:ncols], func=AF.Silu), r=(pk,), w=("tf0",))
                elif nm == "f":
                    aop(lambda: nc.scalar.activation(out=T2[:, 0:ncols], in_=psD[pi][:, 0:ncols], func=AF.Sigmoid), r=(pk,), w=("tf1",))
                    vop(lambda: nc.vector.tensor_scalar(out=T2[:, 0:ncols], in0=T2[:, 0:ncols], scalar1=omlc, scalar2=lbc, op0=ALU.mult, op1=ALU.add),
                        r=("tf1", "const"), w=("tf1",))
                    aop(lambda: nc.scalar.activation(out=T3[:, 0:ncols], in_=T2[:, 0:ncols], func=AF.Ln), r=("tf1",), w=("tf2",))
                    vop(lambda: nc.vector.tensor_scalar(out=T2[:, 0:ncols], in0=T2[:, 0:ncols], scalar1=-1.0, scalar2=1.0, op0=ALU.mult, op1=ALU.add),
                        r=("tf1", "tf2"), w=("tf1",))
                    head_rest(hd)
                    for blk in range(4):
                        pgrp([lambda blk=blk: nc.tensor.transpose(psM[0][:, blk * 128:(blk + 1) * 128], T6[:, blk * 128:(blk + 1) * 128], ident[:])],
                             r=("T6", "const"), w=("psM0",), acc=(blk > 0))
                    aop(lambda: nc.scalar.activation(out=KHTOK, in_=psM[0][:, 0:512].rearrange("p (a b) -> p a b", b=128), func=AF.Copy), r=("psM0",), w=("KHTOK",))
                elif nm == "iv":
                    aop(lambda: nc.scalar.activation(out=T3[:, 0:ncols], in_=psD[pi][:, 0:ncols], func=AF.Copy), r=(pk,), w=("tf2",))
                    for blk in range(4):
                        pgrp([lambda blk=blk: nc.tensor.transpose(psM[1][:, blk * 128:(blk + 1) * 128], T3[:, blk * 128:(blk + 1) * 128], ident[:])],
                             r=("tf2", "const"), w=("psM1",), acc=(blk > 0))
                    aop(lambda: nc.scalar.activation(out=VTOK, in_=psM[1][:, 0:512].rearrange("p (a b) -> p a b", b=128), func=AF.Copy), r=("psM1",), w=("VTOK",))
                    vop(lambda: nc.vector.tensor_scalar(out=VTOKm, in0=VTOK, scalar1=m3[:, 0:1], scalar2=None, op0=ALU.mult),
                        r=("VTOK", "m3"), w=("VTOKm",))
                    if ncols > 512:
                        vop(lambda: nc.vector.tensor_copy(out=VTsb, in_=T3[:, 512:528]), r=("tf2",), w=("VTsb",))
                else:
                    aop(lambda: nc.scalar.activation(out=SG[:, 0:ncols], in_=psD[pi][:, 0:ncols], func=AF.Sigmoid), r=(pk,), w=("SG",))
                    for blk in range(4):
                        cb = slice(blk * 128, (blk + 1) * 128)
                        pgrp([lambda: nc.tensor.matmul(psM[0][:, 0:128], KT[:, cb], QT[:, cb], start=True, stop=True)], r=("KT", "QT"), w=("psM0",))
                        vop(lambda: nc.vector.tensor_tensor(out=ATT, in0=psM[0][:, 0:128], in1=cmask[:], op=ALU.mult), r=("psM0", "const"), w=("ATT",))
                        fns = []
                        for c in range(4):
                            di, co = DSB[c]
                            if c < 3:
                                fns.append(lambda c=c, di=di, co=co: nc.tensor.matmul(psD[di][:, co:co + 128], KHTOK[c * 32:(c + 1) * 32, blk, :],
                                                                                      VTOK[c * 32:(c + 1) * 32, blk, :], start=True, stop=True))
                            else:
                                fns.append(lambda di=di, co=co: nc.tensor.matmul(psD[di][:, co:co + 128], KHTOK[64:128, blk, :],
                                                                                 VTOKm[64:128, blk, :], start=True, stop=True))
                        pgrp(fns, r=("KHTOK", "VTOK", "VTOKm"), w=("psD0", "psD1"))
                        for c in range(4):
                            cg = blk * 4 + c
                            di, co = DSB[c]
                            vop(lambda c=c: nc.vector.tensor_copy(out=SBc[c], in_=Sst), r=("S",), w=(f"SB{c}",))
                            vop(lambda cg=cg, di=di, co=co: nc.vector.scalar_tensor_tensor(out=Sst, in0=Sst, scalar=EBL[:, cg:cg + 1], in1=psD[di][:, co:co + 128],
                                                                                         op0=ALU.mult, op1=ALU.add), r=("S", "EBL", f"psD{di}", f"SB{c}"), w=("S",))
                        pgrp([lambda: nc.tensor.matmul(psD[2][:, cb], VTOK[:, blk, :], ATT, start=True, stop=False, skip_group_check=True)],
                             r=("VTOK", "ATT"), w=("psD2",), acc=(blk > 0))
                        for c in range(4):
                            cc_ = slice(blk * 128 + c * 32, blk * 128 + (c + 1) * 32)
                            pgrp([lambda c=c, cc_=cc_: nc.tensor.matmul(psD[2][:, cc_], SBc[c], QT[:, cc_], start=False, stop=True, skip_group_check=True)],
                                 r=(f"SB{c}", "QT"), w=("psD2",), acc=True)
                    aop(lambda: nc.scalar.activation(out=O[:, 0:512], in_=psD[2][:, 0:512], func=AF.Copy), r=("psD2",), w=("O",))
                    if last_pass:
                        sy.dma(S, o_phg[hd], Sst, reads=("S",), writes=("dram_out",))
                    else:
                        sy.dma(S, hgS[hd], Sst, reads=("S",), writes=("hgS",))
                    if ncols > 512:
                        pgrp([lambda: nc.tensor.matmul(psM[0][:, 128:128 + NS], onesf[:], QK, start=True, stop=True)], r=("QK", "const"), w=("psM0",))
                        vop(lambda: nc.vector.tensor_tensor(out=O[:, 512:528], in0=psM[0][:, 128:128 + NS], in1=T3[:, 512:528], op=ALU.mult),
                            r=("psM0", "tf2"), w=("O",))
                        for half in range(2):
                            n0 = half * 8
                            sy.dma(S, SHG, i_hg[n0:n0 + 8, hd].rearrange("n k v -> k n v"), writes=("SHG",))
                            for nl in range(8):
                                n = n0 + nl
                                pgrp([lambda n=n, nl=nl: nc.tensor.matmul(psM[1][:, n:n + 1], SHG[:, nl, :], QF[:, n:n + 1], start=True, stop=True)],
                                     r=("SHG", "QF"), w=("psM1",), acc=(n > 0))
                                pgrp([lambda n=n: nc.tensor.matmul(psM[0][:, 256:384], VTsb[:, n:n + 1].to_broadcast([128, 128]), identb[:], start=True, stop=True)],
                                     r=("VTsb", "const"), w=("psM0",))
                                vop(lambda n=n, nl=nl: nc.vector.tensor_scalar(out=SHG[:, nl, :], in0=SHG[:, nl, :], scalar1=FS[:, n:n + 1], scalar2=None, op0=ALU.mult),
                                    r=("SHG", "FS", "psM1"), w=("SHG",))
                                vop(lambda n=n, nl=nl: nc.vector.scalar_tensor_tensor(out=SHG[:, nl, :], in0=psM[0][:, 256:384], scalar=KS[:, n:n + 1], in1=SHG[:, nl, :],
                                                                                     op0=ALU.mult, op1=ALU.add), r=("SHG", "KS", "psM0"), w=("SHG",))
                            sy.dma(S, o_shg[n0:n0 + 8, hd].rearrange("n k v -> k n v"), SHG, reads=("SHG",), writes=("dram_out",))
                        vop(lambda: nc.vector.tensor_tensor(out=O[:, 512:528], in0=O[:, 512:528], in1=psM[1][:, 0:NS], op=ALU.add), r=("O", "psM1"), w=("O",))
                    aop(lambda: nc.scalar.activation(out=T5[:, 0:ncols], in_=O[:, 0:ncols], func=AF.Square), r=("O",), w=("T5",))
                    pgrp([lambda: nc.tensor.matmul(psM[0][:, 0:512], onesf[:], T5[:, 0:512], start=True, stop=True)], r=("T5", "const"), w=("psM0",))
                    aop(lambda: nc.scalar.activation(out=T6[:, 0:512], in_=psM[0][:, 0:512], func=AF.Sqrt, bias=epsc[:, 0:1], scale=1.0 / 128), r=("psM0", "const"), w=("T6",))
                    if ncols > 512:
                        pgrp([lambda: nc.tensor.matmul(psM[1][:, 32:32 + NS], onesf[:], T5[:, 512:528], start=True, stop=True)], r=("T5", "const"), w=("psM1",))
                        aop(lambda: nc.scalar.activation(out=T6[:, 512:528], in_=psM[1][:, 32:32 + NS], func=AF.Sqrt, bias=epsc[:, 0:1], scale=1.0 / 128),
                            r=("psM1", "const"), w=("T6",))
                    vop(lambda: nc.vector.reciprocal(out=T6[:, 0:ncols], in_=T6[:, 0:ncols]), r=("T6",), w=("T6",))
                    vop(lambda: nc.vector.scalar_tensor_tensor(out=O[:, 0:ncols], in0=O[:, 0:ncols], scalar=gnc, in1=T6[:, 0:ncols], op0=ALU.mult, op1=ALU.mult),
                        r=("O", "T6", "const"), w=("O",))
                    vop(lambda: nc.vector.tensor_tensor(out=z[:, hd, 0:ncols], in0=O[:, 0:ncols], in1=SG[:, 0:ncols], op=ALU.mult), r=("O", "SG"), w=(f"z{hd}",))
            run_jobs(jobs, comp)
            out_proj_residual(hg_w_out, ncols)
            barrier()

        gop(lambda: nc.gpsimd.memset(ident[:], 0.0), w=("const",))
        gop(lambda: nc.gpsimd.affine_select(out=ident[:], in_=ident[:], compare_op=ALU.not_equal, fill=1.0, base=0,
                                            pattern=[[-1, 128]], channel_multiplier=1), r=("const",), w=("const",))
        vop(lambda: nc.vector.tensor_copy(out=identb[:], in_=ident[:]), r=("const",), w=("const",))
        gop(lambda: nc.gpsimd.memset(ones_b[:], 1.0), w=("const",))
        gop(lambda: nc.gpsimd.memset(onesf[:], 1.0), w=("const",))
        gop(lambda: nc.gpsimd.memset(epsc[:], EPS), w=("const",))
        gop(lambda: nc.gpsimd.memset(ccarry[:], 0.0), w=("ccarry",))
        gop(lambda: nc.gpsimd.memset(msk[:], 1.0), w=("const",))
        gop(lambda: nc.gpsimd.memset(msk[:].rearrange("p (c l) -> p c l", l=32)[:, :, 0:1], 0.0), r=("const",), w=("const",))
        gop(lambda: nc.gpsimd.memset(cmask[:], 1.0), w=("const",))
        cm3 = cmask[:].rearrange("p (a b) -> p a b", b=32)
        gop(lambda: nc.gpsimd.affine_select(out=cm3, in_=cm3, compare_op=ALU.is_ge, fill=0.0, base=0,
                                            pattern=[[-32, 4], [0, 32]], channel_multiplier=1), r=("const",), w=("const",))
        gop(lambda: nc.gpsimd.affine_select(out=cm3, in_=cm3, compare_op=ALU.is_ge, fill=0.0, base=0,
                                            pattern=[[32, 4], [1, 32]], channel_multiplier=-1), r=("const",), w=("const",))
        gop(lambda: nc.gpsimd.affine_select(out=cm3, in_=cm3, compare_op=ALU.is_ge, fill=0.0, base=31,
                                            pattern=[[32, 4], [0, 32]], channel_multiplier=-1), r=("const",), w=("const",))
        load_cols(pc_nmix, norm_mix.rearrange("l (k p) -> (l k) p", p=128), 128)
        load_cols(pc_nffn, norm_ffn.rearrange("l (k p) -> (l k) p", p=128), 128)
        load_cols(pc_nfin, norm_final.rearrange("l (k p) -> (l k) p", p=128), 32)
        load_cols(pc_convw, conv_w.rearrange("l (k p) -> (l k) p", p=128), 96)
        load_cols(pc_s5d, s5_d.rearrange("l (k p) -> (l k) p", p=128), 64)
        load_cols(pc_gn, hg_gn.rearrange("l (k p) -> (l k) p", p=128), 32)
        lg = tf[0][:, 0:128]
        load_cols(lg, hg_lb.rearrange("l (k p) -> (l k) p", p=128), 128)
        mxl = tf[1][:, 0:32]
        vop(lambda: nc.vector.tensor_tensor(out=mxl, in0=lg[:, 0:32], in1=lg[:, 32:64], op=ALU.max), r=("const",), w=("const",))
        vop(lambda: nc.vector.tensor_tensor(out=mxl, in0=mxl, in1=lg[:, 64:96], op=ALU.max), r=("const",), w=("const",))
        vop(lambda: nc.vector.tensor_tensor(out=mxl, in0=mxl, in1=lg[:, 96:128], op=ALU.max), r=("const",), w=("const",))
        for l4 in range(4):
            vop(lambda l4=l4: nc.vector.tensor_tensor(out=lg[:, l4 * 32:(l4 + 1) * 32], in0=lg[:, l4 * 32:(l4 + 1) * 32], in1=mxl, op=ALU.subtract), r=("const",), w=("const",))
        aop(lambda: nc.scalar.activation(out=lg, in_=lg, func=AF.Exp), r=("const",), w=("const",))
        sm = tf[1][:, 32:64]; nm12 = tf[1][:, 64:96]
        vop(lambda: nc.vector.tensor_tensor(out=nm12, in0=lg[:, 32:64], in1=lg[:, 64:96], op=ALU.add), r=("const",), w=("const",))
        vop(lambda: nc.vector.tensor_tensor(out=sm, in0=lg[:, 0:32], in1=lg[:, 96:128], op=ALU.add), r=("const",), w=("const",))
        vop(lambda: nc.vector.tensor_tensor(out=sm, in0=sm, in1=nm12, op=ALU.add), r=("const",), w=("const",))
        vop(lambda: nc.vector.reciprocal(out=sm, in_=sm), r=("const",), w=("const",))
        vop(lambda: nc.vector.tensor_tensor(out=pc_lb[:], in0=nm12, in1=sm, op=ALU.mult), r=("const",), w=("const",))
        vop(lambda: nc.vector.tensor_scalar(out=pc_oml[:], in0=pc_lb[:], scalar1=-1.0, scalar2=1.0, op0=ALU.mult, op1=ALU.add), r=("const",), w=("const",))
        if 0 in layers:
            s5_setup(0)
        if 3 in layers:
            s5_setup(1)

        for p in range(npass):
            ncols = 528 if p == 0 else 512
            for tt in range(4):
                for half in range(2):
                    r0 = p * TP + tt * 128
                    sy.dma(S, ost_flat, xp[r0:r0 + 128, half * 2048:(half + 1) * 2048], writes=TFK)
                    for g4 in range(4):
                        pm = g4 % 2
                        fns = []
                        for q4 in range(4):
                            fns.append(lambda g4=g4, q4=q4, pm=pm: nc.tensor.transpose(psM[pm][:, q4 * 128:(q4 + 1) * 128], ostage[:, g4, q4 * 128:(q4 + 1) * 128], ident[:]))
                        pgrp(fns, r=TFK + ("const",), w=(f"psM{pm}",))
                        kc0 = half * 16 + g4 * 4
                        aop(lambda kc0=kc0, pm=pm, tt=tt: nc.scalar.activation(out=h[:, kc0:kc0 + 4, tt * 128:(tt + 1) * 128],
                                                                               in_=psM[pm][:, :].rearrange("p (a b) -> p a b", b=128), func=AF.Copy),
                            r=(f"psM{pm}",), w=tuple(f"h{kc0 + i}" for i in range(4)))
            if p == 0:
                for half in range(2):
                    sy.dma(S, ost_flat[0:NS, :], xs[:, half * 2048:(half + 1) * 2048], writes=TFK)
                    fns = []
                    for q in range(16):
                        fns.append(lambda q=q: nc.tensor.transpose(psM[1][:, q * NS:(q + 1) * NS], ost_flat[0:NS, q * 128:(q + 1) * 128], ident[0:NS, 0:NS]))
                    pgrp(fns, r=TFK + ("const",), w=("psM1",))
                    vop(lambda half=half: nc.vector.tensor_copy(out=h[:, half * 16:(half + 1) * 16, 512:528],
                                                                in_=psM[1][:, 0:16 * NS].rearrange("p (a b) -> p a b", b=NS)),
                        r=("psM1",), w=tuple(f"h{half * 16 + i}" for i in range(16)))

            for layer in range(4):
                kind = layer % 3
                rmsnorm(lambda kc, layer=layer: pc_nmix[:, layer * 32 + kc:layer * 32 + kc + 1], ncols)
                if layer in layers:
                    if kind == 0:
                        s5_mixer(layer // 3, layer, p, ncols)
                    elif kind == 1:
                        conv_mixer(p, ncols)
                    else:
                        hgrn_mixer(p, ncols, p == npass - 1)
                if do_ffn:
                    rmsnorm(lambda kc, layer=layer: pc_nffn[:, layer * 32 + kc:layer * 32 + kc + 1], ncols)
                    ffn(layer, ncols)

            rmsnorm(lambda kc: pc_nfin[:, kc:kc + 1], ncols, make_u=False)
            for kc in range(KC):
                vop(lambda kc=kc: nc.vector.scalar_tensor_tensor(out=h[:, kc, 0:ncols], in0=h[:, kc, 0:ncols], scalar=pc_nfin[:, kc:kc + 1],
                                                                 in1=rstd[:, 0:ncols], op0=ALU.mult, op1=ALU.mult),
                    r=(f"h{kc}", "rstd", "const"), w=(f"h{kc}",))
            for tt in range(4):
                r0 = p * TP + tt * 128
                transpose_cols_out(lambda kc, tt=tt: h[:, kc, tt * 128:(tt + 1) * 128], 128, yp[r0:r0 + 128, :], hkeys)
            if p == 0:
                transpose_cols_out(lambda kc: h[:, kc, 512:528], NS, ys[:, :], hkeys)

        pgrp([lambda: nc.tensor.transpose(psM[0][0:64, 0:128], ccarry[:].rearrange("p a b -> p (a b)"), ident[:])], r=("ccarry", "const"), w=("psM0",))
        vop(lambda: nc.vector.tensor_copy(out=pstage[0:64, :], in_=psM[0][0:64, 0:128]), r=("psM0",), w=("pstage",))
        sy.dma(S, o_pconv.rearrange("r (k p) -> (r k) p", p=128), pstage[0:64, :], reads=("pstage",), writes=("dram_out",))
        for j in range(2):
            for r, od in ((0, o_pre), (1, o_pim)):
                pgrp([lambda j=j, r=r: nc.tensor.transpose(psM[0][:, 0:128], hst[j][r][:], ident[:])], r=(f"hst{j}", "const"), w=("psM0",))
                vop(lambda: nc.vector.tensor_copy(out=pstage[:, :], in_=psM[0][:, 0:128]), r=("psM0",), w=("pstage",))
                sy.dma(S, od[j].rearrange("(i jj) p -> i (jj p)", jj=2), pstage[:, :], reads=("pstage",), writes=("dram_out",))

        barrier()
    return nc


_CACHE = {}


def prep_inputs(inp):
    f32 = np.float32
    def tile_in(W):
        K_, N_ = W.shape
        return np.ascontiguousarray(W.reshape(K_ // 128, 128, N_ // 128, 128).transpose(2, 1, 0, 3)).reshape(N_, K_)

    shared = {}
    for k in ("norm_mix", "norm_ffn", "s5_a_re", "s5_a_im", "s5_log_dt", "s5_b_re", "s5_b_im", "s5_c_re", "s5_c_im",
              "s5_d", "hgrn_lb_logits", "hgrn_gnorm"):
        shared[k] = np.ascontiguousarray(inp[k], dtype=f32)
    shared["norm_final"] = np.ascontiguousarray(inp["norm_final"], dtype=f32).reshape(1, D)
    shared["conv_w"] = np.ascontiguousarray(inp["conv_w"], dtype=f32).reshape(3, D)
    shared["s5_w_glu_t"] = np.stack([tile_in(np.asarray(inp["s5_w_glu"][j], dtype=f32)) for j in range(2)], 0)
    shared["conv_w_in_t"] = tile_in(np.asarray(inp["conv_w_in"][0], dtype=f32))
    shared["conv_w_out_t"] = tile_in(np.asarray(inp["conv_w_out"][0], dtype=f32))
    shared["hgrn_w_in_t"] = tile_in(np.asarray(inp["hgrn_w_in"][0], dtype=f32))
    shared["hgrn_w_out_t"] = tile_in(np.asarray(inp["hgrn_w_out"][0], dtype=f32))
    shared["ffn_w_in_t"] = np.stack([tile_in(np.asarray(inp["ffn_w_in"][l], dtype=f32)) for l in range(4)], 0)
    wo = np.asarray(inp["ffn_w_out"], dtype=f32)
    shared["ffn_w_out_ta"] = np.stack([np.stack([tile_in(wo[l, b * D:(b + 1) * D]) for b in range(2)], 0) for l in range(4)], 0)
    shared["ffn_w_out_tb"] = np.stack([tile_in(wo[l, 2 * D:]) for l in range(4)], 0)
    in_maps = []
    for c in range(8):
        m = dict(shared)
        m["xp"] = np.ascontiguousarray(inp["x_prompt"][c % 4], dtype=f32)
        sl = slice(c * NS, (c + 1) * NS)
        m["xs"] = np.ascontiguousarray(inp["x_sample"][sl, 0, :], dtype=f32)
        m["i_s5re"] = np.ascontiguousarray(inp["state_s5_re"][:, sl], dtype=f32)
        m["i_s5im"] = np.ascontiguousarray(inp["state_s5_im"][:, sl], dtype=f32)
        m["i_conv"] = np.ascontiguousarray(inp["state_conv"][0, sl], dtype=f32)
        m["i_hg"] = np.ascontiguousarray(inp["state_hgrn"][0, sl], dtype=f32)
        in_maps.append(m)
    return in_maps


def kernel(**inp):
    f32 = np.float32
    nc = _CACHE.get("nc")
    if nc is None:
        nc = build()
        _CACHE["nc"] = nc
    in_maps = prep_inputs(inp)
    res = run_bass_kernel_spmd(nc, in_maps, core_ids=list(range(8)))
    R = res.results
    y_prompt = np.stack([R[c]["yp"] for c in range(4)], 0)
    y_sample = np.concatenate([R[c]["ys"] for c in range(8)], 0)[:, None, :]
    p_re = np.stack([R[c]["o_pre"] for c in range(4)], 1)
    p_im = np.stack([R[c]["o_pim"] for c in range(4)], 1)
    p_conv = np.stack([R[c]["o_pconv"] for c in range(4)], 0)[None]
    p_hg = np.stack([R[c]["o_phg"] for c in range(4)], 0)[None]
    s_re = np.concatenate([R[c]["o_sre"] for c in range(8)], 1)
    s_im = np.concatenate([R[c]["o_sim"] for c in range(8)], 1)
    s_conv = np.concatenate([R[c]["o_sconv"] for c in range(8)], 0)[None]
    s_hg = np.concatenate([R[c]["o_shg"] for c in range(8)], 0)[None]
    return (y_prompt.astype(f32), y_sample.astype(f32), p_re.astype(f32), p_im.astype(f32), p_conv.astype(f32),
            p_hg.astype(f32), s_re.astype(f32), s_im.astype(f32), s_conv.astype(f32), s_hg.astype(f32))
```
